# Optimizing a Trainium2 kernel written in Bass

```python
import math
import jax
import jax.numpy as jnp
from jax import lax
import numpy as np

D_MODEL = 1024
BATCH = 8
SEQ = 4096
DEPTH = 1

D_HEAD = 64
NSA_HEADS = 8
NSA_KV_HEADS = 2
NSA_GROUP = NSA_HEADS // NSA_KV_HEADS
D_NSA = NSA_HEADS * D_HEAD
D_KV = NSA_KV_HEADS * D_HEAD
N_BRANCH = 3
N_GATES = NSA_HEADS * N_BRANCH
CMP_BLOCK = 32
CMP_STRIDE = 16
CMP_HIDDEN = 256
SEL_BLOCK = 64
SEL_TOPK = 16
WINDOW = 512
Q_BLOCK = 64
GMLP_GROUPS = 8
GMLP_GROUP_DIM = 64
D_GMLP = GMLP_GROUPS * GMLP_GROUP_DIM
CHUNK = 128
D_MIX = D_NSA + D_GMLP
D_PROJ = D_NSA + 6 * D_KV + N_GATES + 2 * D_GMLP
SPLITS = (D_NSA, D_NSA + D_KV, D_NSA + 2 * D_KV, D_NSA + 3 * D_KV, D_NSA + 4 * D_KV, D_NSA + 5 * D_KV, D_NSA + 6 * D_KV, D_NSA + 6 * D_KV + N_GATES)
REL_BUCKETS = 32
REL_MAX_DIST = 128
D_FF = 2816
CONV_WIDTH = 3
EPS = 1e-6
NEG_INF = -1e30
FORCED_SCORE = 1e4

kernel_name = "hybrid_nsa_sgu_convffn_block"


def _rms_norm(x, g):
    xf = x.astype(jnp.float32)
    y = xf * lax.rsqrt(jnp.mean(xf * xf, axis=-1, keepdims=True) + EPS)
    return (y * g.astype(jnp.float32)).astype(x.dtype)


def _rel_bucket(dist):
    max_exact = REL_BUCKETS // 2
    d = jnp.maximum(dist, 1).astype(jnp.float32)
    log_b = max_exact + (jnp.log(d / max_exact) / math.log(REL_MAX_DIST / max_exact) * (REL_BUCKETS - max_exact)).astype(jnp.int32)
    log_b = jnp.clip(log_b, max_exact, REL_BUCKETS - 1)
    return jnp.where(dist < max_exact, jnp.maximum(dist, 0), log_b)


def _bias_shared(dist, rel_bias):
    b = rel_bias[_rel_bucket(dist)]
    return jnp.transpose(b, (2, 0, 1)).reshape(NSA_KV_HEADS, NSA_GROUP, dist.shape[0], dist.shape[1]).astype(jnp.float32)


def _bias_gathered(dist, rel_bias):
    table = rel_bias.T.reshape(NSA_KV_HEADS, NSA_GROUP, REL_BUCKETS)
    gi = jnp.arange(NSA_KV_HEADS)[None, :, None, None, None]
    ri = jnp.arange(NSA_GROUP)[None, None, :, None, None]
    return table[gi, ri, _rel_bucket(dist)[:, :, None]].astype(jnp.float32)


def _heads(t, B, T):
    return jnp.transpose(t.reshape(B, T, NSA_KV_HEADS, D_HEAD), (0, 2, 1, 3))


def _compress(tok, pe, w1, b1, w2):
    B, G, T, D = tok.shape
    nc = (T - CMP_BLOCK) // CMP_STRIDE + 1
    idx = jnp.arange(nc)[:, None] * CMP_STRIDE + jnp.arange(CMP_BLOCK)[None, :]
    blk = (tok[:, :, idx] + pe).reshape(B, G, nc, CMP_BLOCK * D)
    return jax.nn.gelu(blk @ w1 + b1) @ w2


def _nsa(q, k_cmp, v_cmp, k_slc, v_slc, k_win, v_win, gates, rel_bias):
    B, G, R, T, D = q.shape
    nc = k_cmp.shape[2]
    ns = T // SEL_BLOCK
    n_sel = min(SEL_TOPK, ns)
    cmp_end = jnp.arange(nc) * CMP_STRIDE + (CMP_BLOCK - 1)
    cs = jnp.arange(nc)[:, None] * CMP_STRIDE
    ss = jnp.arange(ns)[None, :] * SEL_BLOCK
    overlap = ((cs < ss + SEL_BLOCK) & (cs + CMP_BLOCK > ss)).astype(jnp.float32)
    blk_start = jnp.arange(ns) * SEL_BLOCK
    k_blocks = k_slc.reshape(B, G, ns, SEL_BLOCK, D)
    v_blocks = v_slc.reshape(B, G, ns, SEL_BLOCK, D)
    k_pad = jnp.pad(k_win, ((0, 0), (0, 0), (WINDOW, 0), (0, 0)))
    v_pad = jnp.pad(v_win, ((0, 0), (0, 0), (WINDOW, 0), (0, 0)))
    bi = jnp.arange(B)[:, None, None, None]
    gi = jnp.arange(G)[None, :, None, None]

    def block(qi):
        t0 = qi * Q_BLOCK
        tpos = t0 + jnp.arange(Q_BLOCK)
        qb = lax.dynamic_slice_in_dim(q, t0, Q_BLOCK, axis=3)
        gb = lax.dynamic_slice_in_dim(gates, t0, Q_BLOCK, axis=3)

        dist_c = tpos[:, None] - cmp_end[None, :]
        valid_c = dist_c >= 0
        s_c = jnp.einsum('bgrqd,bgnd->bgrqn', qb, k_cmp).astype(jnp.float32) + _bias_shared(dist_c, rel_bias)
        p_c = jnp.where(valid_c, jax.nn.softmax(jnp.where(valid_c, s_c, NEG_INF), axis=-1), 0.0)
        o_c = jnp.einsum('bgrqn,bgnd->bgrqd', p_c.astype(v_cmp.dtype), v_cmp)

        imp = jnp.einsum('bgrqn,ns->bgqs', p_c, overlap)
        cur = tpos[:, None] // SEL_BLOCK
        j = jnp.arange(ns)[None, :]
        forced = (j == 0) | (j == cur) | (j == cur - 1)
        causal_blk = blk_start[None, :] <= tpos[:, None]
        score = jnp.where(forced, FORCED_SCORE, jnp.where(causal_blk, imp, NEG_INF))
        _, sel = lax.top_k(score, n_sel)

        k_sel = k_blocks[bi, gi, sel].reshape(B, G, Q_BLOCK, n_sel * SEL_BLOCK, D)
        v_sel = v_blocks[bi, gi, sel].reshape(B, G, Q_BLOCK, n_sel * SEL_BLOCK, D)
        kpos = (sel[..., None] * SEL_BLOCK + jnp.arange(SEL_BLOCK)).reshape(B, G, Q_BLOCK, n_sel * SEL_BLOCK)
        dist_s = tpos[None, None, :, None] - kpos
        s_s = jnp.einsum('bgrqd,bgqkd->bgrqk', qb, k_sel).astype(jnp.float32) + _bias_gathered(dist_s, rel_bias)
        p_s = jax.nn.softmax(jnp.where((dist_s >= 0)[:, :, None], s_s, NEG_INF), axis=-1)
        o_s = jnp.einsum('bgrqk,bgqkd->bgrqd', p_s.astype(v_sel.dtype), v_sel)

        kw = lax.dynamic_slice_in_dim(k_pad, t0, Q_BLOCK + WINDOW, axis=2)
        vw = lax.dynamic_slice_in_dim(v_pad, t0, Q_BLOCK + WINDOW, axis=2)
        kpos_w = t0 - WINDOW + jnp.arange(Q_BLOCK + WINDOW)
        dist_w = tpos[:, None] - kpos_w[None, :]
        valid_w = (dist_w >= 0) & (dist_w < WINDOW) & (kpos_w[None, :] >= 0)
        s_w = jnp.einsum('bgrqd,bgkd->bgrqk', qb, kw).astype(jnp.float32) + _bias_shared(dist_w, rel_bias)
        p_w = jax.nn.softmax(jnp.where(valid_w, s_w, NEG_INF), axis=-1)
        o_w = jnp.einsum('bgrqk,bgkd->bgrqd', p_w.astype(vw.dtype), vw)

        return gb[..., 0:1] * o_c + gb[..., 1:2] * o_s + gb[..., 2:3] * o_w

    o = lax.map(block, jnp.arange(T // Q_BLOCK))
    return jnp.transpose(o, (1, 0, 4, 2, 3, 5)).reshape(B, T, G * R * D)


def _sgu(uv, sgu_norm, sgu_w, sgu_b):
    B, T, _ = uv.shape
    u, v = jnp.split(jax.nn.gelu(uv), 2, axis=-1)
    v = _rms_norm(v, sgu_norm).reshape(B, T // CHUNK, CHUNK, GMLP_GROUPS, GMLP_GROUP_DIM)
    w = sgu_w * jnp.tril(jnp.ones((CHUNK, CHUNK), sgu_w.dtype))
    z = jnp.einsum('gij,bcjge->bcige', w, v) + sgu_b.T[:, :, None]
    return u * z.reshape(B, T, D_GMLP)


def _conv_ffn(h, w_up, conv_w, conv_b, w_down):
    up = h @ w_up
    up = lax.conv_general_dilated(up, conv_w[:, None, :], (1,), [(CONV_WIDTH - 1, 0)], dimension_numbers=('NWC', 'WIO', 'NWC'), feature_group_count=up.shape[-1]) + conv_b
    a, g = jnp.split(up, 2, axis=-1)
    return (jax.nn.silu(g) * a) @ w_down


def setup_inputs(seed: int = 0) -> dict:
    key = jax.random.key(seed)
    ks = jax.random.split(key, 32)
    L = DEPTH

    def nrm(k, shape, scale):
        return jax.random.normal(k, shape, jnp.float32) * scale

    def gain(k, shape):
        return 1.0 + 0.01 * jax.random.normal(k, shape, jnp.float32)

    return {
        "x": nrm(ks[0], (BATCH, SEQ, D_MODEL), 1.0),
        "rel_bias": nrm(ks[1], (REL_BUCKETS, NSA_HEADS), 0.5),
        "attn_norm": gain(ks[2], (L, D_MODEL)),
        "w_in": nrm(ks[3], (L, D_MODEL, D_PROJ), D_MODEL ** -0.5),
        "q_norm": gain(ks[4], (L, D_HEAD)),
        "k_norm_cmp": gain(ks[5], (L, D_HEAD)),
        "k_norm_slc": gain(ks[6], (L, D_HEAD)),
        "k_norm_win": gain(ks[7], (L, D_HEAD)),
        "cmp_pe_k": nrm(ks[8], (L, CMP_BLOCK, D_HEAD), 0.1),
        "cmp_w1_k": nrm(ks[9], (L, CMP_BLOCK * D_HEAD, CMP_HIDDEN), (CMP_BLOCK * D_HEAD) ** -0.5),
        "cmp_b1_k": nrm(ks[10], (L, CMP_HIDDEN), 0.01),
        "cmp_w2_k": nrm(ks[11], (L, CMP_HIDDEN, D_HEAD), CMP_HIDDEN ** -0.5),
        "cmp_pe_v": nrm(ks[12], (L, CMP_BLOCK, D_HEAD), 0.1),
        "cmp_w1_v": nrm(ks[13], (L, CMP_BLOCK * D_HEAD, CMP_HIDDEN), (CMP_BLOCK * D_HEAD) ** -0.5),
        "cmp_b1_v": nrm(ks[14], (L, CMP_HIDDEN), 0.01),
        "cmp_w2_v": nrm(ks[15], (L, CMP_HIDDEN, D_HEAD), CMP_HIDDEN ** -0.5),
        "sgu_norm": gain(ks[16], (L, D_GMLP)),
        "sgu_w": nrm(ks[17], (L, GMLP_GROUPS, CHUNK, CHUNK), CHUNK ** -0.5),
        "sgu_b": gain(ks[18], (L, GMLP_GROUPS, CHUNK)),
        "w_out": nrm(ks[19], (L, D_MIX, D_MODEL), D_MIX ** -0.5),
        "ffn_norm": gain(ks[20], (L, D_MODEL)),
        "w_up": nrm(ks[21], (L, D_MODEL, 2 * D_FF), D_MODEL ** -0.5),
        "conv_w": nrm(ks[22], (L, CONV_WIDTH, 2 * D_FF), CONV_WIDTH ** -0.5),
        "conv_b": nrm(ks[23], (L, 2 * D_FF), 0.01),
        "w_down": nrm(ks[24], (L, D_FF, D_MODEL), D_FF ** -0.5),
    }


def reference(x, rel_bias, attn_norm, w_in, q_norm, k_norm_cmp, k_norm_slc, k_norm_win,
              cmp_pe_k, cmp_w1_k, cmp_b1_k, cmp_w2_k, cmp_pe_v, cmp_w1_v, cmp_b1_v, cmp_w2_v,
              sgu_norm, sgu_w, sgu_b, w_out, ffn_norm, w_up, conv_w, conv_b, w_down):
    B, T, _ = x.shape
    G, R = NSA_KV_HEADS, NSA_GROUP
    for l in range(DEPTH):
        h = _rms_norm(x, attn_norm[l])
        proj = h @ w_in[l]
        q, kc, vc, ks_, vs, kw, vw, gl, uv = jnp.split(proj, SPLITS, axis=-1)
        q = _rms_norm(q.reshape(B, T, G, R, D_HEAD), q_norm[l]) * (D_HEAD ** -0.5)
        q = jnp.transpose(q, (0, 2, 3, 1, 4))
        k_cmp = _rms_norm(_compress(_heads(kc, B, T), cmp_pe_k[l], cmp_w1_k[l], cmp_b1_k[l], cmp_w2_k[l]), k_norm_cmp[l])
        v_cmp = _compress(_heads(vc, B, T), cmp_pe_v[l], cmp_w1_v[l], cmp_b1_v[l], cmp_w2_v[l])
        k_slc = _rms_norm(_heads(ks_, B, T), k_norm_slc[l])
        v_slc = _heads(vs, B, T)
        k_win = _rms_norm(_heads(kw, B, T), k_norm_win[l])
        v_win = _heads(vw, B, T)
        gates = jnp.transpose(jax.nn.sigmoid(gl.reshape(B, T, G, R, N_BRANCH)), (0, 2, 3, 1, 4))
        o_nsa = _nsa(q, k_cmp, v_cmp, k_slc, v_slc, k_win, v_win, gates, rel_bias)
        o_sgu = _sgu(uv, sgu_norm[l], sgu_w[l], sgu_b[l])
        x = x + jnp.concatenate([o_nsa, o_sgu], axis=-1) @ w_out[l]
        x = x + _conv_ffn(_rms_norm(x, ffn_norm[l]), w_up[l], conv_w[l], conv_b[l], w_down[l])
    return x
```

```python
import contextlib
import os
import numpy as np
import concourse.bass as bass
import concourse.mybir as mybir
from concourse.bass_utils import run_bass_kernel_spmd

F32 = mybir.dt.float32
BF16 = mybir.dt.bfloat16
AF = mybir.ActivationFunctionType
ALU = mybir.AluOpType
AX = mybir.AxisListType

T = 4096
D = 1024
NT = T // 128
DFF = 2816
NFC = DFF // 128
EPS = 1e-6
NEG = -30000.0
DF = 384

ENGS = ("pe", "act", "dve", "pool", "sp")


class Tok:
    __slots__ = ("w", "r")

    def __init__(self):
        self.w = None
        self.r = []


class Prog:
    NDMA = 12

    def __init__(self, nc):
        self.nc = nc
        self.ops = []
        self.last = {e: None for e in ENGS}
        self.dmas = []

    def op(self, eng, fn, reads=(), writes=(), dma=False, nss=False, extra=()):
        import os
        if fn is not None and len(self.ops) >= int(os.environ.get("MAXOPS", "100000000")):
            return -1
        oid = len(self.ops)
        deps = set(extra)
        smode = os.environ.get("SERIAL", "1")
        if smode == "1" or (smode == "2" and not dma) or (smode == "3"):
            chain = (smode != "3") or dma or getattr(self, "prev_was_dma", False)
            if chain and getattr(self, "prev_real", None) is not None:
                deps.add(self.prev_real)
            if smode != "3":
                nss = False
            if fn is not None:
                self.prev_real = oid
                self.prev_was_dma = dma
        deps.discard(oid)
        if fn is not None and not dma:
            if eng == "pool" and self.last["dve"] is not None:
                deps.add(self.last["dve"])
            if eng == "dve" and self.last["pool"] is not None:
                deps.add(self.last["pool"])
        self.ops.append(dict(eng=eng, fn=fn, deps=deps, dma=dma, nss=nss))
        for t in reads:
            t.r.append(oid)
        for t in writes:
            t.w = oid
            t.r = []
        if dma:
            self.dmas.append(oid)
        else:
            if fn is not None:
                self.last[eng] = oid
        return oid

    def pe(self, fn, reads=(), writes=()):
        return self.op("pe", fn, reads, writes, nss=True)

    def act(self, fn, reads=(), writes=()):
        return self.op("act", fn, reads, writes)

    def dve(self, fn, reads=(), writes=()):
        return self.op("dve", fn, reads, writes)

    def pool(self, fn, reads=(), writes=()):
        return self.op("pool", fn, reads, writes)

    def dma(self, fn, reads=(), writes=(), q="sp"):
        return self.op(q, fn, reads, writes, dma=True)

    def barrier(self):
        deps = [v for v in self.last.values() if v is not None] + list(self.dmas)
        self.dmas = []
        for e in ENGS:
            self.op(e, None, extra=deps)

    def emit(self):
        nc = self.nc
        ops = self.ops
        n = len(ops)

        def skip(o, od):
            return od["eng"] == o["eng"] and not o["dma"] and o["nss"] and o["fn"] is not None

        needed = [False] * n
        for o in ops:
            for d in o["deps"]:
                od = ops[d]
                if od["dma"] or skip(o, od):
                    continue
                needed[d] = True
        cnt = {e: 0 for e in ENGS}
        signo = [0] * n
        dma_cnt = {"sp": 0, "pool": 0}
        dma_slot = [None] * n
        for i, o in enumerate(ops):
            if o["dma"]:
                di = dma_cnt[o["eng"]]
                dma_slot[i] = ((o["eng"], di % self.NDMA), 16 * (di // self.NDMA + 1))
                dma_cnt[o["eng"]] += 1
            elif needed[i]:
                cnt[o["eng"]] += 1
                signo[i] = cnt[o["eng"]]
        self.stats = dict(cnt=dict(cnt), ndma=dict(dma_cnt), nops=n)
        es = contextlib.ExitStack()
        with es:
            sems = {e: es.enter_context(nc.semaphore("s_" + e)) for e in ENGS}
            dsems = {(q, k): es.enter_context(nc.semaphore("d%s_%d" % (q, k))) for q in ("sp", "pool") for k in range(self.NDMA)}
            block = es.enter_context(nc.Block())
            per_eng = {e: [] for e in ENGS}
            seen = {e: {} for e in ENGS}
            for i, o in enumerate(ops):
                e = o["eng"]
                waits = []

                def need(key, semh, val):
                    if seen[e].get(key, 0) >= val:
                        return
                    seen[e][key] = val
                    waits.append((semh, val))

                for d in sorted(o["deps"]):
                    od = ops[d]
                    if od["dma"]:
                        k, v = dma_slot[d]
                        need(("d", k), dsems[k], v)
                    else:
                        if skip(o, od):
                            continue
                        need(("e", od["eng"]), sems[od["eng"]], signo[d])
                inc = None
                if o["dma"]:
                    k, v = dma_slot[i]
                    if v > 16:
                        need(("d", k), dsems[k], v - 16)
                    inc = (dsems[k], 16)
                elif needed[i]:
                    inc = (sems[e], 1)
                per_eng[e].append((waits, o["fn"], inc))
            tail = []
            for q in ("sp", "pool"):
                for k in range(self.NDMA):
                    if dma_cnt[q] > k:
                        tail.append((dsems[(q, k)], 16 * ((dma_cnt[q] - 1 - k) // self.NDMA + 1)))
            tail += [(sems[e], cnt[e]) for e in ENGS if cnt[e] > 0]

            def runner(ename):
                def body(eng):
                    for waits, fn, inc in per_eng[ename]:
                        for semh, val in waits:
                            eng.wait_ge(semh, val)
                        if fn is None:
                            assert inc is None
                            continue
                        ins = fn(eng)
                        if inc is not None:
                            ins.then_inc(inc[0], inc[1])
                    if ename == "sp":
                        for semh, val in tail:
                            eng.wait_ge(semh, val)
                return body

            block.tensor(runner("pe"))
            block.scalar(runner("act"))
            block.vector(runner("dve"))
            block.gpsimd(runner("pool"))
            block.sync(runner("sp"))


def MM(out, lhsT, rhs, start=True, stop=True):
    return lambda e: e.matmul(out, lhsT=lhsT, rhs=rhs, start=start, stop=stop)


def TR(out, in_, ident):
    return lambda e: e.transpose(out=out, in_=in_, identity=ident)


def ACT(out, in_, func, **kw):
    return lambda e: e.activation(out=out, in_=in_, func=func, **kw)


def TT(out, a, b, op):
    return lambda e: e.tensor_tensor(out=out, in0=a, in1=b, op=op)


def TS(out, a, s1, s2, op0, op1=None):
    if op1 is None:
        return lambda e: e.tensor_scalar(out=out, in0=a, scalar1=s1, scalar2=None, op0=op0)
    return lambda e: e.tensor_scalar(out=out, in0=a, scalar1=s1, scalar2=s2, op0=op0, op1=op1)


def STT(out, a, s, b, op0, op1):
    return lambda e: e.scalar_tensor_tensor(out=out, in0=a, scalar=s, in1=b, op0=op0, op1=op1)


def CP(out, in_):
    return lambda e: e.tensor_copy(out=out, in_=in_)


def DMA(out, in_, **kw):
    return lambda e: e.dma_start(out=out, in_=in_, **kw)


def DMAN(out, in_):
    return lambda e: e.dma_start(out=out, in_=in_, allow_slow_non_contiguous=True)


def MEMSET(ap, v):
    return lambda e: e.memset(ap, v)


def RSUM(out, in_):
    return lambda e: e.reduce_sum(out=out, in_=in_, axis=AX.X)


class Arena:
    def __init__(self, nc, words):
        self.t = nc.alloc_sbuf_tensor("arena", [128, words], F32)
        self.words = words
        self.off = 0

    def alloc(self, free, dt):
        n = int(np.prod(free))
        w = n if dt == F32 else (n + 1) // 2
        w = (w + 7) // 8 * 8
        assert self.off + w <= self.words, ("SBUF arena overflow", self.off, w, self.words)
        v = self.t[:, self.off:self.off + w]
        self.off += w
        if dt != F32:
            v = v.bitcast(dt)
        v = v[:, 0:n]
        if len(free) == 2:
            v = v.rearrange("p (a b) -> p a b", a=free[0])
        elif len(free) == 3:
            v = v.rearrange("p (a b c) -> p a b c", a=free[0], b=free[1])
        return v


def _bucket_np(dist):
    dist = np.asarray(dist, dtype=np.int64)
    d = np.maximum(dist, 1).astype(np.float32)
    lg = (np.log(d / np.float32(16)) / np.float32(np.log(128 / 16)) * np.float32(16)).astype(np.float32)
    log_b = 16 + lg.astype(np.int32)
    log_b = np.clip(log_b, 16, 31)
    return np.where(dist < 16, np.maximum(dist, 0), log_b)


def host_consts():
    c = {}
    c["c_ident"] = np.eye(128, dtype=np.float32)
    c["c_anti"] = np.eye(128, dtype=np.float32)[::-1].copy()
    ohd = np.zeros((33, DF), np.float32)
    for idx in range(DF):
        d = idx - 128
        if d < 0:
            ohd[32, idx] = NEG
        else:
            ohd[_bucket_np(d), idx] += 1.0
            ohd[31, idx] -= 1.0
    c["c_ohd"] = ohd
    eb = np.zeros((64, T), np.float32)
    for s in range(64):
        eb[s, s * 64:(s + 1) * 64] = 1.0
    c["c_eband"] = eb
    db = np.zeros((16, 376), np.float32)
    for k in range(16):
        db[k, k + 239] = 1.0
    c["c_dband"] = db
    ov = np.zeros((256, 64), np.float32)
    for n in range(255):
        for s in range(64):
            if 16 * n < 64 * s + 64 and 16 * n + 32 > 64 * s:
                ov[n, s] = 1.0
    c["c_ov"] = ov
    ar = np.zeros((128, 127), np.float32)
    for q in range(128):
        hi = 1 if q >= 64 else 0
        for j in range(127):
            sp = j - 63
            if sp > hi:
                ar[q, j] = -1e30
            elif sp == hi or sp == hi - 1:
                ar[q, j] = 8.0
    c["c_arel"] = ar
    ed = np.full((128, 128), NEG, np.float32)
    for k in range(128):
        ed[k, :k] = 0.0
    c["c_edge"] = ed
    c["c_tril"] = np.tril(np.ones((128, 128), np.float32))
    return c


CONST_SHAPES = dict(c_ident=[128, 128], c_anti=[128, 128], c_ohd=[33, DF], c_eband=[64, T], c_dband=[16, 376],
                    c_ov=[256, 64], c_arel=[128, 127], c_edge=[128, 128], c_tril=[128, 128])

IN_SHAPES = dict(
    x=[T, D], rel_bias=[32, 8], attn_norm=[D], w_in=[D, 2328], q_norm=[64], k_norm_cmp=[64], k_norm_slc=[64],
    k_norm_win=[64], cmp_pe_k=[32, 64], cmp_w1_k=[2048, 256], cmp_b1_k=[256], cmp_w2_k=[256, 64],
    cmp_pe_v=[32, 64], cmp_w1_v=[2048, 256], cmp_b1_v=[256], cmp_w2_v=[256, 64], sgu_norm=[512],
    sgu_w=[8, 128, 128], sgu_b=[8, 128], w_out=[D, D], ffn_norm=[D], w_up=[D, 2 * DFF], conv_w=[3, 2 * DFF],
    conv_b=[2 * DFF], w_down=[DFF, D])


def build_nc(dbg=False, stop_after=4, p1_tiles=NT):
    nc = bass.Bass("TRN2", target_bir_lowering=False)
    I = {k: nc.dram_tensor(k, s, F32, kind="ExternalInput") for k, s in IN_SHAPES.items()}
    C = {k: nc.dram_tensor(k, s, F32, kind="ExternalInput") for k, s in CONST_SHAPES.items()}
    out = nc.dram_tensor("out", [T, D], F32, kind="ExternalOutput")
    x1s = nc.dram_tensor("x1s", [T, D], F32, kind="ExternalOutput" if dbg else "Internal")
    fscr = nc.dram_tensor("fscr", [8, DF], F32, kind="Internal")
    dmix = nc.dram_tensor("dmix", [T, D], BF16, kind="ExternalOutput") if dbg else None
    dkc = nc.dram_tensor("dkc", [128, 256], BF16, kind="ExternalOutput") if dbg else None
    dvc = nc.dram_tensor("dvc", [128, 260], BF16, kind="ExternalOutput") if dbg else None

    P = Prog(nc)
    A = Arena(nc, 52000)
    PS = [nc.alloc_psum_tensor("ps%d" % i, [128, 512], F32) for i in range(8)]
    PSB = [p.bitcast(BF16) for p in PS]
    tps = [Tok() for _ in range(8)]

    def bcast_dram(t, n):
        return bass.AP(t, 0, [[0, 128], [1, n]])

    identb = A.alloc([128], BF16); t_identb = Tok()
    identf = A.alloc([128], F32); t_identf = Tok()
    nhalf = A.alloc([16], F32); t_nhalf = Tok()
    QT = A.alloc([4, T], BF16); t_QT = [Tok() for _ in range(NT)]
    KST = A.alloc([T], BF16); t_KST = [Tok() for _ in range(NT)]
    KWT = A.alloc([T], BF16); t_KWT = [Tok() for _ in range(NT)]
    KCc = A.alloc([T], BF16); t_KCc = [Tok() for _ in range(NT)]
    VCc = A.alloc([T], BF16); t_VCc = [Tok() for _ in range(NT)]
    Vs = A.alloc([NT, 2, 65], BF16); t_Vs = [Tok() for _ in range(NT)]
    Vw = A.alloc([NT, 2, 65], BF16); t_Vw = [Tok() for _ in range(NT)]
    gates = A.alloc([NT, 24], F32); t_gates = [Tok() for _ in range(NT)]
    OSGT = A.alloc([4, T], BF16); t_OSGT = [Tok() for _ in range(NT)]
    KCT = A.alloc([256], BF16); t_KCT = Tok()
    Vc = A.alloc([2, 2, 65], BF16); t_Vc = Tok()
    gfc = A.alloc([64], F32); t_gfc = Tok()
    GF4 = A.alloc([4, 64], F32); t_GF4 = Tok()
    persist_mark = A.off

    P.dma(DMA(identf, C["c_ident"].ap()), writes=[t_identf])
    P.dma(DMA(identb, C["c_ident"].ap()), writes=[t_identb], q="pool")
    P.pool(MEMSET(nhalf, -0.5), writes=[t_nhalf])
    t_ones = Tok()
    P.pool(MEMSET(Vs[:, :, :, 64:65], 1.0), writes=[t_ones])
    P.pool(MEMSET(Vw[:, :, :, 64:65], 1.0), writes=[t_ones])
    P.pool(MEMSET(Vc, 0.0), writes=[t_Vc])
    P.pool(MEMSET(Vc[:, :, :, 64:65], 1.0), writes=[t_Vc])
    P.pool(MEMSET(KCT, 0.0), writes=[t_KCT])
    gq = A.alloc([64], F32); gk = A.alloc([3, 64], F32); t_g = Tok()
    P.dma(DMA(gq, bcast_dram(I["q_norm"], 64)), writes=[t_g])
    for i, nm in enumerate(("k_norm_cmp", "k_norm_slc", "k_norm_win")):
        P.dma(DMA(gk[:, i, :], bcast_dram(I[nm], 64)), writes=[t_g])
    P.dve(STT(gfc, gk[:, 0, :], 0.125, gq, ALU.mult, ALU.mult), reads=[t_g], writes=[t_gfc])
    for j, i in enumerate((1, 1, 2, 2)):
        P.dve(STT(GF4[:, j, :], gk[:, i, :], 0.125, gq, ALU.mult, ALU.mult), reads=[t_g], writes=[t_GF4])

    def rsqrt_mean(ss, n_out, nmean, tmp, out, t_in, t_out):
        P.dve(TS(tmp, ss, 1.0 / nmean, EPS, ALU.mult, ALU.add), reads=t_in, writes=t_out)
        P.pool(TT(out, tmp, nhalf[:, 0:n_out], ALU.pow), reads=t_out + [t_nhalf], writes=t_out)

    ph_mark = A.off
    Win = A.alloc([8, 2328], BF16); t_Win = Tok()
    gA = A.alloc([D], F32); t_gA = Tok()
    gSG = A.alloc([512], F32); t_gSG = Tok()
    WT = A.alloc([8, 128], BF16); t_WT = Tok()
    sbT = A.alloc([8], F32); t_sbT = Tok()
    XT = [A.alloc([D], F32) for _ in range(2)]; t_XT = [Tok(), Tok()]
    hb = A.alloc([D], BF16); t_hb = Tok()
    hT = A.alloc([8, 128], BF16); t_hT = Tok()
    qf = A.alloc([512], F32); t_qf = Tok()
    sq = A.alloc([512], F32); t_sq = Tok()
    kf = A.alloc([256], F32); t_kf = Tok()
    qb16 = A.alloc([512], BF16); t_qb16 = Tok()
    kb16 = A.alloc([256], BF16); t_kb16 = Tok()
    cb16 = A.alloc([256], BF16); t_cb16 = Tok()
    uf = A.alloc([512], F32); t_uf = Tok()
    vf = A.alloc([512], F32); t_vf = Tok()
    vb = A.alloc([512], BF16); t_vb = Tok()
    zf = sq; t_zf = t_sq
    osg = A.alloc([512], BF16); t_osg = Tok()
    sm = A.alloc([64], F32); t_sm = Tok()
    gt = A.alloc([24], F32); t_gt = Tok()
    sgw = A.alloc([128], F32); t_sgw = Tok()
    tril = A.alloc([128], F32); t_tril = Tok()

    w_in_v = I["w_in"].ap().rearrange("(k p) n -> p k n", p=128)
    col = 0
    segs = []
    for r in range(4):
        for g in range(2):
            segs.append(((g * 4 + r) * 64, 64))
    segs += [(768, 128), (1024, 128), (512, 128), (640, 128)]
    segs += [(896, 128), (1152, 128), (1280, 24)]
    segs += [(1304, 1024)]
    for (c0, n) in segs:
        P.dma(DMA(Win[:, :, col:col + n], w_in_v[:, :, c0:c0 + n]), writes=[t_Win], q="pool")
        col += n
    assert col == 2328
    P.dma(DMA(gA, bcast_dram(I["attn_norm"], D)), writes=[t_gA])
    P.dma(DMA(gSG, bcast_dram(I["sgu_norm"], 512)), writes=[t_gSG])
    P.dma(DMA(tril, C["c_tril"].ap()), writes=[t_tril])
    P.dma(DMAN(sbT, I["sgu_b"].ap().rearrange("g i -> i g")), writes=[t_sbT])
    for g in range(8):
        P.dma(DMA(sgw, I["sgu_w"].ap()[g]), writes=[t_sgw])
        P.dve(TT(sgw, sgw, tril, ALU.mult), reads=[t_sgw, t_tril], writes=[t_sgw])
        P.pe(TR(PS[7][:, 0:128], sgw, identf), reads=[t_sgw, t_identf], writes=[tps[7]])
        P.act(ACT(WT[:, g, :], PS[7][:, 0:128], AF.Copy), reads=[tps[7]], writes=[t_WT])

    CH = [(0, 512), (512, 512), (1024, 280), (1304, 512), (1816, 512)]
    print("ops before tiles", len(P.ops))
    for qb in range(p1_tiles):
        t0 = qb * 128
        print("tile", qb, "starts at op", len(P.ops))
        xt = XT[qb % 2]; t_xt = t_XT[qb % 2]
        P.dma(DMA(xt, I["x"].ap()[t0:t0 + 128, :]), writes=[t_xt])
        P.act(ACT(hb, xt, AF.Square, accum_out=sm[:, 0:1]), reads=[t_xt], writes=[t_hb, t_sm])
        rsqrt_mean(sm[:, 0:1], 1, D, sm[:, 1:2], sm[:, 2:3], [t_sm], [t_sm])
        P.dve(STT(hb, xt, sm[:, 2:3], gA, ALU.mult, ALU.mult), reads=[t_xt, t_sm, t_gA], writes=[t_hb])
        for k in range(8):
            P.pe(TR(PSB[0][:, k * 128:(k + 1) * 128], hb[:, k * 128:(k + 1) * 128], identb),
                 reads=[t_hb, t_identb], writes=[tps[0]])
        P.act(ACT(hT, PSB[0][:, :].rearrange("p (k t) -> p k t", k=8), AF.Copy), reads=[tps[0]], writes=[t_hT])
        for ci, (c0, n) in enumerate(CH):
            for k in range(8):
                P.pe(MM(PS[1 + ci][:, 0:n], hT[:, k, :], Win[:, k, c0:c0 + n], k == 0, k == 7),
                     reads=[t_hT, t_Win], writes=[tps[1 + ci]])
        P.act(ACT(qf, PS[1][:, :], AF.Copy), reads=[tps[1]], writes=[t_qf])
        P.dve(TT(sq, qf, qf, ALU.mult), reads=[t_qf], writes=[t_sq])
        P.dve(RSUM(sm[:, 8:16], sq.rearrange("p (h d) -> p h d", h=8)), reads=[t_sq], writes=[t_sm])
        P.act(ACT(kf, PS[2][:, 0:256], AF.Copy), reads=[tps[2]], writes=[t_kf])
        P.dve(TT(sq[:, 0:256], kf, kf, ALU.mult), reads=[t_kf], writes=[t_sq])
        P.dve(RSUM(sm[:, 16:20], sq[:, 0:256].rearrange("p (h d) -> p h d", h=4)), reads=[t_sq], writes=[t_sm])
        rsqrt_mean(sm[:, 8:20], 12, 64, sm[:, 20:32], sm[:, 32:44], [t_sm], [t_sm])
        P.dve(TT(qb16.rearrange("p (h d) -> p h d", h=8), qf.rearrange("p (h d) -> p h d", h=8),
                 sm[:, 32:40].unsqueeze(2).to_broadcast([128, 8, 64]), ALU.mult), reads=[t_qf, t_sm], writes=[t_qb16])
        P.dve(TT(kf.rearrange("p (h d) -> p h d", h=4), kf.rearrange("p (h d) -> p h d", h=4),
                 sm[:, 40:44].unsqueeze(2).to_broadcast([128, 4, 64]), ALU.mult), reads=[t_kf, t_sm], writes=[t_kf])
        P.dve(TT(kb16, kf, GF4.rearrange("p a b -> p (a b)"), ALU.mult), reads=[t_kf, t_GF4], writes=[t_kb16])
        P.act(ACT(cb16, PS[2][:, 256:512], AF.Copy), reads=[tps[2]], writes=[t_cb16])
        for j in range(4):
            P.pe(TR(PSB[7][:, j * 128:(j + 1) * 128], qb16[:, j * 128:(j + 1) * 128], identb),
                 reads=[t_qb16, t_identb], writes=[tps[7]])
        for j in range(2):
            P.pe(TR(PSB[7][:, 512 + j * 128:640 + j * 128], kb16[:, j * 128:(j + 1) * 128], identb),
                 reads=[t_kb16, t_identb], writes=[tps[7]])
            P.pe(TR(PSB[7][:, 768 + j * 128:896 + j * 128], cb16[:, j * 128:(j + 1) * 128], identb),
                 reads=[t_cb16, t_identb], writes=[tps[7]])
        P.act(ACT(QT[:, :, t0:t0 + 128], PSB[7][:, 0:512].rearrange("p (j t) -> p j t", j=4), AF.Copy),
              reads=[tps[7]], writes=[t_QT[qb]])
        P.dve(CP(KST[:, t0:t0 + 128], PSB[7][:, 512:640]), reads=[tps[7]], writes=[t_KST[qb]])
        P.dve(CP(KWT[:, t0:t0 + 128], PSB[7][:, 640:768]), reads=[tps[7]], writes=[t_KWT[qb]])
        P.act(ACT(KCc[:, t0:t0 + 128], PSB[7][:, 768:896], AF.Copy), reads=[tps[7]], writes=[t_KCc[qb]])
        P.act(ACT(VCc[:, t0:t0 + 128], PSB[7][:, 896:1024], AF.Copy), reads=[tps[7]], writes=[t_VCc[qb]])
        P.act(ACT(Vs[:, qb, :, 0:64], PS[3][:, 0:128].rearrange("p (g d) -> p g d", g=2), AF.Copy),
              reads=[tps[3], t_ones], writes=[t_Vs[qb]])
        P.act(ACT(Vw[:, qb, :, 0:64], PS[3][:, 128:256].rearrange("p (g d) -> p g d", g=2), AF.Copy),
              reads=[tps[3], t_ones], writes=[t_Vw[qb]])
        P.act(ACT(gt, PS[3][:, 256:280], AF.Tanh, scale=0.5), reads=[tps[3]], writes=[t_gt])
        P.dve(TS(gates[:, qb, :], gt, 0.5, 0.5, ALU.mult, ALU.add), reads=[t_gt], writes=[t_gates[qb]])
        P.act(ACT(uf, PS[4][:, :], AF.Gelu_apprx_tanh), reads=[tps[4]], writes=[t_uf])
        P.act(ACT(vf, PS[5][:, :], AF.Gelu_apprx_tanh), reads=[tps[5]], writes=[t_vf])
        P.act(ACT(zf, vf, AF.Square, accum_out=sm[:, 48:49]), reads=[t_vf], writes=[t_zf, t_sm])
        rsqrt_mean(sm[:, 48:49], 1, 512, sm[:, 49:50], sm[:, 50:51], [t_sm], [t_sm])
        P.dve(STT(vb, vf, sm[:, 50:51], gSG, ALU.mult, ALU.mult), reads=[t_vf, t_sm, t_gSG], writes=[t_vb])
        for g in range(8):
            P.pe(MM(PS[6][:, g * 64:(g + 1) * 64], WT[:, g, :], vb[:, g * 64:(g + 1) * 64]),
                 reads=[t_WT, t_vb], writes=[tps[6]])
        P.dve(TT(zf.rearrange("p (g e) -> p g e", g=8), PS[6][:, :].rearrange("p (g e) -> p g e", g=8),
                 sbT.unsqueeze(2).to_broadcast([128, 8, 64]), ALU.add), reads=[tps[6], t_sbT], writes=[t_zf])
        P.dve(TT(osg, zf, uf, ALU.mult), reads=[t_zf, t_uf], writes=[t_osg])
        for j in range(4):
            P.pe(TR(PSB[0][:, j * 128:(j + 1) * 128], osg[:, j * 128:(j + 1) * 128], identb),
                 reads=[t_osg, t_identb], writes=[tps[0]])
        P.act(ACT(OSGT[:, :, t0:t0 + 128], PSB[0][:, 0:512].rearrange("p (j t) -> p j t", j=4), AF.Copy),
              reads=[tps[0]], writes=[t_OSGT[qb]])
        if dbg:
            P.dma(DMA(dmix[t0:t0 + 128, 512:1024], osg), reads=[t_osg])

    P.barrier()
    if stop_after == 1:
        P.emit()
        return nc, P

    A.off = ph_mark
    W1 = A.alloc([32, 256], BF16); t_W1 = Tok()
    W2 = A.alloc([2, 64], BF16); t_W2 = Tok()
    peT = A.alloc([32], BF16); t_peT = Tok()
    b1 = A.alloc([2], F32); t_b1 = Tok()
    cst = A.alloc([2], F32); t_cst = Tok()
    hidT = A.alloc([2, 256], BF16); t_hid = Tok()
    kcf = A.alloc([2, 2, 64], F32); t_kcf = Tok()
    kcs = A.alloc([2, 2, 64], F32); t_kcs = Tok()
    kcb = A.alloc([2, 128], BF16); t_kcb = Tok()
    sm2 = A.alloc([16], F32); t_sm2 = Tok()
    P.pool(MEMSET(hidT, 0.0), writes=[t_hid])
    P.pool(MEMSET(kcf, 0.0), writes=[t_kcf])
    for kv in ("k", "v"):
        src = KCc if kv == "k" else VCc
        t_src = t_KCc if kv == "k" else t_VCc
        w1v = I["cmp_w1_" + kv].ap().rearrange("(j d) h -> d j h", d=64)
        for half in range(2):
            P.dma(DMA(W1[half * 64:(half + 1) * 64, :, :], w1v), writes=[t_W1], q="pool")
        P.dma(DMA(W2, I["cmp_w2_" + kv].ap().rearrange("(h p) d -> p h d", p=128)), writes=[t_W2], q="pool")
        P.dma(DMAN(peT[0:64, :], I["cmp_pe_" + kv].ap().rearrange("j d -> d j")), writes=[t_peT], q="pool")
        P.dma(DMAN(b1, I["cmp_b1_" + kv].ap().rearrange("(h p) -> p h", p=128)), writes=[t_b1])
        for half in range(2):
            for j in range(32):
                P.pe(MM(PS[0][:, half:half + 1], W1[0:64, j, half * 128:(half + 1) * 128], peT[0:64, j:j + 1],
                        j == 0, j == 31), reads=[t_W1, t_peT], writes=[tps[0]])
        P.dve(TT(cst, PS[0][:, 0:2], b1, ALU.add), reads=[tps[0], t_b1], writes=[t_cst])
        for g in range(2):
            for half in range(2):
                bank = 1 + half
                for j in range(32):
                    P.pe(MM(PS[bank][:, 0:255], W1[g * 64:(g + 1) * 64, j, half * 128:(half + 1) * 128],
                            src[g * 64:(g + 1) * 64, j:j + 4065:16], j == 0, j == 31),
                         reads=[t_W1] + t_src, writes=[tps[bank]])
                P.act(ACT(hidT[:, half, 0:255], PS[bank][:, 0:255], AF.Gelu_apprx_tanh, bias=cst[:, half:half + 1]),
                      reads=[tps[bank], t_cst], writes=[t_hid])
            for nt in range(2):
                M = 128 if nt == 0 else 127
                for half in range(2):
                    P.pe(MM(PS[3 + nt][0:M, 0:64], hidT[:, half, nt * 128:nt * 128 + M], W2[:, half, :],
                            half == 0, half == 1), reads=[t_hid, t_W2], writes=[tps[3 + nt]])
                if kv == "v":
                    P.act(ACT(Vc[0:M, nt, g, 0:64], PS[3 + nt][0:M, 0:64], AF.Copy), reads=[tps[3 + nt]], writes=[t_Vc])
                else:
                    P.act(ACT(kcf[0:M, nt, g, :], PS[3 + nt][0:M, 0:64], AF.Copy), reads=[tps[3 + nt]], writes=[t_kcf])
        if kv == "k":
            P.dve(TT(kcs, kcf, kcf, ALU.mult), reads=[t_kcf], writes=[t_kcs])
            P.dve(RSUM(sm2[:, 0:4], kcs.rearrange("p a b c -> p (a b) c")), reads=[t_kcs], writes=[t_sm2])
            rsqrt_mean(sm2[:, 0:4], 4, 64, sm2[:, 4:8], sm2[:, 8:12], [t_sm2], [t_sm2])
            P.dve(TT(kcf.rearrange("p a b c -> p (a b) c"), kcf.rearrange("p a b c -> p (a b) c"),
                     sm2[:, 8:12].unsqueeze(2).to_broadcast([128, 4, 64]), ALU.mult), reads=[t_kcf, t_sm2], writes=[t_kcf])
            P.dve(TT(kcb.rearrange("p a (b c) -> p (a b) c", b=2), kcf.rearrange("p a b c -> p (a b) c"),
                     gfc.unsqueeze(1).to_broadcast([128, 4, 64]), ALU.mult), reads=[t_kcf, t_gfc], writes=[t_kcb])
            for nt in range(2):
                M = 128 if nt == 0 else 127
                P.pe(TR(PSB[5][:, nt * 128:nt * 128 + M], kcb[0:M, nt, :], identb[0:M, 0:M]),
                     reads=[t_kcb, t_identb], writes=[tps[5]])
            P.act(ACT(KCT[:, 0:255], PSB[5][:, 0:255], AF.Copy), reads=[tps[5]], writes=[t_KCT])
    if dbg:
        P.dma(DMA(dkc.ap(), KCT), reads=[t_KCT])
        P.dma(DMA(dvc.ap(), Vc.rearrange("p a b c -> p (a b c)")), reads=[t_Vc])
    P.barrier()
    if stop_after == 2:
        P.emit()
        return nc, P

    A.off = ph_mark
    Wout = A.alloc([8, D], BF16); t_Wout = Tok()
    Eband = A.alloc([T], BF16); t_Eb = Tok()
    dband = A.alloc([376], BF16); t_db = Tok()
    ovb = A.alloc([2, 64], BF16); t_ov = Tok()
    arel = A.alloc([127], F32); t_arel = Tok()
    edge = A.alloc([128], BF16); t_edge = Tok()
    anti = A.alloc([128], F32); t_anti = Tok()
    rbx = A.alloc([8], F32); t_rbx = Tok()
    ohd = A.alloc([DF], F32); t_ohd = Tok()
    Fs = A.alloc([DF], F32); t_Fs = Tok()
    Hk = A.alloc([8, 128], F32); t_Hk = Tok()
    BTd = A.alloc([8, 128], BF16); BTo = A.alloc([8, 128], BF16); BTc = A.alloc([8, 128], BF16); t_BT = Tok()
    PTs = [A.alloc([512], BF16) for _ in range(4)]; t_PT = [Tok() for _ in range(4)]
    XT3 = [A.alloc([D], F32) for _ in range(2)]; t_XT3 = [Tok(), Tok()]
    X1 = [A.alloc([D], F32) for _ in range(2)]; t_X1 = [Tok(), Tok()]
    ONSA = A.alloc([512], BF16); t_ONSA = Tok()
    MIXT = A.alloc([4, 128], BF16); t_MIXT = Tok()
    NMT = A.alloc([128], BF16); t_NMT = Tok()
    negm = A.alloc([64], BF16); t_negm = Tok()
    e1 = A.alloc([4, 64], F32); t_e1 = Tok()
    e2 = A.alloc([4, 64], F32); t_e2 = Tok()
    score = A.alloc([64], F32); t_score = Tok()
    wk = A.alloc([64], F32); t_wk = Tok()
    m8 = A.alloc([16], F32); t_m8 = Tok()
    st = A.alloc([32], F32); t_st = Tok()

    for c0 in range(0, 8, 2):
        P.dma(DMA(Wout[:, c0:c0 + 2, :], I["w_out"].ap().rearrange("(k p) n -> p k n", p=128)[:, c0:c0 + 2, :]),
              writes=[t_Wout], q="pool")
    P.dma(DMA(Eband[0:64, :], C["c_eband"].ap()), writes=[t_Eb], q="pool")
    P.dma(DMA(dband[0:16, :], C["c_dband"].ap()), writes=[t_db], q="pool")
    P.dma(DMA(ovb, C["c_ov"].ap().rearrange("(t p) s -> p t s", p=128)), writes=[t_ov], q="pool")
    P.dma(DMA(arel, C["c_arel"].ap()), writes=[t_arel])
    P.dma(DMA(edge, C["c_edge"].ap()), writes=[t_edge], q="pool")
    P.dma(DMA(anti, C["c_anti"].ap()), writes=[t_anti])
    P.pool(MEMSET(rbx[32:33, :], 1.0), writes=[t_rbx])
    P.dma(DMA(rbx[0:32, :], I["rel_bias"].ap()), writes=[t_rbx])
    P.dma(DMA(ohd[0:33, :], C["c_ohd"].ap()), writes=[t_ohd])
    P.pe(MM(PS[0][0:8, 0:DF], rbx[0:33, 0:8], ohd[0:33, :]), reads=[t_rbx, t_ohd], writes=[tps[0]])
    P.act(ACT(Fs[0:8, :], PS[0][0:8, 0:DF], AF.Copy), reads=[tps[0]], writes=[t_Fs])
    P.dma(DMA(fscr.ap(), Fs[0:8, :]), reads=[t_Fs], writes=[t_Fs])
    for (c0, BT, rows, pstep) in ((1, BTd, 128, 1), (129, BTo, 128, 1), (1, BTc, 16, 16)):
        P.dma(DMA(Hk[0:rows, :, :], bass.AP(fscr, c0, [[pstep, rows], [DF, 8], [1, 128]])), reads=[t_Fs], writes=[t_Hk])
        for hh in range(2):
            lhsT = anti if rows == 128 else anti[0:16, 112:128]
            P.pe(MM(PS[1 + hh][0:rows, :], lhsT, Hk[0:rows, hh * 4:(hh + 1) * 4, :]), reads=[t_anti, t_Hk],
                 writes=[tps[1 + hh]])
            P.act(ACT(BT[0:rows, hh * 4:(hh + 1) * 4, :], PS[1 + hh][0:rows, :].rearrange("p (h q) -> p h q", h=4),
                      AF.Copy), reads=[tps[1 + hh]], writes=[t_BT])

    sbank = [0, 1, 2]
    sctr = [0]
    pctr = [0]

    def score_exp(mms, Mv, reads):
        b = sbank[sctr[0] % 3]; sctr[0] += 1
        pi = pctr[0] % 4; pctr[0] += 1
        for i, (lhsT, rhs) in enumerate(mms):
            P.pe(MM(PS[b][0:Mv, :].rearrange("p (r q) -> p r q", r=4), lhsT, rhs, i == 0, i == len(mms) - 1),
                 reads=reads, writes=[tps[b]])
        P.act(ACT(PTs[pi][0:Mv, :], PS[b][0:Mv, :], AF.Exp), reads=[tps[b]], writes=[t_PT[pi]])
        return PTs[pi], t_PT[pi]

    for qb in range(NT):
        t0 = qb * 128
        xt = XT3[qb % 2]; t_xt = t_XT3[qb % 2]
        P.dma(DMA(xt, I["x"].ap()[t0:t0 + 128, :]), writes=[t_xt])
        for g in range(2):
            gs = slice(g * 64, (g + 1) * 64)
            qrhs = QT[gs, :, t0:t0 + 128]
            nkt = 1 if 8 * qb + 7 <= 128 else 2
            for nt in range(nkt):
                Mv = min(128, 8 * qb + 7 - 128 * nt)
                s = 128 * nt - 8 * qb + 9 + 239
                PT, tPT = score_exp([(KCT[gs, nt * 128:nt * 128 + Mv], qrhs),
                                     (dband[0:16, s:s + Mv], BTc[0:16, g * 4:(g + 1) * 4, :])], Mv,
                                    [t_KCT, t_QT[qb], t_db, t_BT])
                for r in range(4):
                    st_, sp_ = (nt == 0 and r == 0), (nt == nkt - 1 and r == 3)
                    P.pe(MM(PS[3][:, r * 65:(r + 1) * 65], PT[0:Mv, r * 128:(r + 1) * 128], Vc[0:Mv, nt, g, :],
                            st_, sp_), reads=[tPT, t_Vc], writes=[tps[3]])
                    P.pe(MM(PS[4][:, r * 64:(r + 1) * 64], PT[0:Mv, r * 128:(r + 1) * 128], ovb[0:Mv, nt, :],
                            st_, sp_), reads=[tPT, t_ov], writes=[tps[4]])
            Oc = PS[3][:, 0:260].rearrange("p (r e) -> p r e", r=4)
            P.dve(TS(st[:, 0:4], Oc[:, :, 64], 1e-30, None, ALU.max), reads=[tps[3]], writes=[t_st])
            P.dve(lambda e, o=st[:, 0:4]: e.reciprocal(out=o, in_=o), reads=[t_st], writes=[t_st])
            P.dve(TT(e1, PS[4][:, 0:256].rearrange("p (r s) -> p r s", r=4),
                     st[:, 0:4].unsqueeze(2).to_broadcast([128, 4, 64]), ALU.mult), reads=[tps[4], t_st], writes=[t_e1])
            P.dve(RSUM(score, e1.rearrange("p r s -> p s r")), reads=[t_e1], writes=[t_score])
            P.dve(TT(score, score, arel[:, 63 - 2 * qb:127 - 2 * qb], ALU.add), reads=[t_score, t_arel], writes=[t_score])
            P.dve(TS(score[:, 0:1], score[:, 0:1], 8.0, None, ALU.add), reads=[t_score], writes=[t_score])
            P.dve(lambda e: e.max(out=m8[:, 0:8], in_=score), reads=[t_score], writes=[t_m8])
            P.dve(lambda e: e.match_replace(out=wk, in_to_replace=m8[:, 0:8], in_values=score, imm_value=-3.0e38),
                  reads=[t_m8, t_score], writes=[t_wk])
            P.dve(lambda e: e.max(out=m8[:, 8:16], in_=wk), reads=[t_wk], writes=[t_m8])
            P.dve(TS(negm, score, m8[:, 15:16], NEG, ALU.is_lt, ALU.mult), reads=[t_score, t_m8], writes=[t_negm])
            P.pe(TR(PSB[7][0:64, 0:128], negm, identb), reads=[t_negm, t_identb], writes=[tps[7]])
            P.act(ACT(NMT[0:64, :], PSB[7][0:64, 0:128], AF.Copy), reads=[tps[7]], writes=[t_NMT])
            nmrhs = NMT[0:64, :].unsqueeze(1).to_broadcast([64, 4, 128])
            for kb in range(qb + 1):
                ks = slice(kb * 128, (kb + 1) * 128)
                mms = [(KST[gs, ks], qrhs)]
                if kb == qb:
                    mms.append((identb, BTd[:, g * 4:(g + 1) * 4, :]))
                else:
                    if kb == qb - 1:
                        mms.append((identb, BTo[:, g * 4:(g + 1) * 4, :]))
                    mms.append((Eband[0:64, ks], nmrhs))
                PT, tPT = score_exp(mms, 128, [t_KST[kb], t_QT[qb], t_identb, t_BT, t_Eb, t_NMT])
                for r in range(4):
                    P.pe(MM(PS[5][:, r * 65:(r + 1) * 65], PT[:, r * 128:(r + 1) * 128], Vs[:, kb, g, :],
                            kb == 0 and r == 0, kb == qb and r == 3), reads=[tPT, t_Vs[kb], t_ones], writes=[tps[5]])
            kb0 = max(0, qb - 4)
            for kb in range(kb0, qb + 1):
                ks = slice(kb * 128, (kb + 1) * 128)
                mms = [(KWT[gs, ks], qrhs)]
                if kb == qb:
                    mms.append((identb, BTd[:, g * 4:(g + 1) * 4, :]))
                elif kb == qb - 1:
                    mms.append((identb, BTo[:, g * 4:(g + 1) * 4, :]))
                elif kb == qb - 4:
                    mms.append((identb, edge.unsqueeze(1).to_broadcast([128, 4, 128])))
                PT, tPT = score_exp(mms, 128, [t_KWT[kb], t_QT[qb], t_identb, t_BT, t_edge])
                for r in range(4):
                    P.pe(MM(PS[6][:, r * 65:(r + 1) * 65], PT[:, r * 128:(r + 1) * 128], Vw[:, kb, g, :],
                            kb == kb0 and r == 0, kb == qb and r == 3), reads=[tPT, t_Vw[kb], t_ones], writes=[tps[6]])
            Os = PS[5][:, 0:260].rearrange("p (r e) -> p r e", r=4)
            Ow = PS[6][:, 0:260].rearrange("p (r e) -> p r e", r=4)
            P.dve(lambda e, o=st[:, 4:8], i=Os[:, :, 64]: e.reciprocal(out=o, in_=i), reads=[tps[5]], writes=[t_st])
            P.dve(lambda e, o=st[:, 8:12], i=Ow[:, :, 64]: e.reciprocal(out=o, in_=i), reads=[tps[6]], writes=[t_st])
            gv = gates[:, qb, g * 12:(g + 1) * 12].rearrange("p (r b) -> p b r", b=3)
            P.dve(TT(st[:, 12:24].rearrange("p (b r) -> p b r", b=3), st[:, 0:12].rearrange("p (b r) -> p b r", b=3),
                     gv, ALU.mult), reads=[t_st, t_gates[qb]], writes=[t_st])
            P.dve(TT(e1, Oc[:, :, 0:64], st[:, 12:16].unsqueeze(2).to_broadcast([128, 4, 64]), ALU.mult),
                  reads=[tps[3], t_st], writes=[t_e1])
            P.dve(TT(e2, Os[:, :, 0:64], st[:, 16:20].unsqueeze(2).to_broadcast([128, 4, 64]), ALU.mult),
                  reads=[tps[5], t_st], writes=[t_e2])
            P.dve(TT(e1, e1, e2, ALU.add), reads=[t_e1, t_e2], writes=[t_e1])
            P.dve(TT(e2, Ow[:, :, 0:64], st[:, 20:24].unsqueeze(2).to_broadcast([128, 4, 64]), ALU.mult),
                  reads=[tps[6], t_st], writes=[t_e2])
            P.dve(TT(ONSA[:, g * 256:(g + 1) * 256].rearrange("p (r d) -> p r d", r=4), e1, e2, ALU.add),
                  reads=[t_e1, t_e2], writes=[t_ONSA])
        if dbg:
            P.dma(DMA(dmix[t0:t0 + 128, 0:512], ONSA), reads=[t_ONSA])
        for j in range(4):
            P.pe(TR(PSB[7][:, j * 128:(j + 1) * 128], ONSA[:, j * 128:(j + 1) * 128], identb),
                 reads=[t_ONSA, t_identb], writes=[tps[7]])
        P.act(ACT(MIXT, PSB[7][:, 0:512].rearrange("p (j t) -> p j t", j=4), AF.Copy), reads=[tps[7]], writes=[t_MIXT])
        x1 = X1[qb % 2]; t_x1 = t_X1[qb % 2]
        for nh in range(2):
            bank = 3 + nh
            for c in range(8):
                lhsT = MIXT[:, c, :] if c < 4 else OSGT[:, c - 4, t0:t0 + 128]
                P.pe(MM(PS[bank][:, :], lhsT, Wout[:, c, nh * 512:(nh + 1) * 512], c == 0, c == 7),
                     reads=[t_MIXT, t_OSGT[qb], t_Wout], writes=[tps[bank]])
            P.dve(TT(x1[:, nh * 512:(nh + 1) * 512], PS[bank][:, :], xt[:, nh * 512:(nh + 1) * 512], ALU.add),
                  reads=[tps[bank], t_xt], writes=[t_x1])
        P.dma(DMA(x1s.ap()[t0:t0 + 128, :], x1), reads=[t_x1])

    P.barrier()
    if stop_after == 3:
        P.emit()
        return nc, P

    A.off = persist_mark if False else 0
    identb4 = A.alloc([128], BF16)
    identf4 = A.alloc([128], F32); nhalf4 = A.alloc([16], F32)
    Wup = A.alloc([8, 2 * DFF], BF16); t_Wup = Tok()
    Wdn = A.alloc([NFC, D], BF16); t_Wdn = Tok()
    gF = A.alloc([D], F32); t_gF = Tok()
    cwT = A.alloc([176], F32); t_cw = Tok(); t_cb = t_cw
    cw = cwT[:, 0:132].rearrange("p (w c) -> p w c", w=3)
    cbias = cwT[:, 132:176]
    S1 = A.alloc([128], F32); S2 = A.alloc([128], F32); t_S = Tok()
    XF = [A.alloc([D], F32) for _ in range(4)]; t_XF = [Tok() for _ in range(4)]
    hb4 = [A.alloc([D], BF16) for _ in range(2)]; t_hb4 = [Tok(), Tok()]
    h2T = [A.alloc([8, 256], BF16) for _ in range(2)]; t_h2T = [Tok(), Tok()]
    halo = A.alloc([2 * NFC, 2], F32); t_halo = [Tok() for _ in range(2 * NFC)]
    Ub = [A.alloc([258], F32) for _ in range(4)]; t_Ub = [Tok() for _ in range(4)]
    acc = [A.alloc([2, 256], F32) for _ in range(2)]; t_acc = [Tok(), Tok()]
    sg = [A.alloc([256], F32) for _ in range(2)]; t_sg = [Tok(), Tok()]
    actT = [A.alloc([256], BF16) for _ in range(3)]; t_actT = [Tok() for _ in range(3)]
    OT = [A.alloc([D], F32) for _ in range(2)]; t_OT = [Tok(), Tok()]
    sm4 = A.alloc([8], F32); t_sm4 = Tok()

    wupv = I["w_up"].ap().rearrange("(k p) n -> p k n", p=128)
    for k in range(8):
        P.dma(DMA(Wup[:, k, :], wupv[:, k, :]), writes=[t_Wup], q="pool")
    wdnv = I["w_down"].ap().rearrange("(c p) n -> p c n", p=128)
    for c in range(0, NFC, 2):
        P.dma(DMA(Wdn[:, c:c + 2, :], wdnv[:, c:c + 2, :]), writes=[t_Wdn], q="pool")
    P.dma(DMA(gF, bcast_dram(I["ffn_norm"], D)), writes=[t_gF])
    cwv = I["conv_w"].ap().rearrange("w (c p) -> (w c) p", p=128)
    P.dma(DMA(S1, cwv[0:128, :]), writes=[t_S])
    P.dma(DMA(S2[0:4, :], cwv[128:132, :]), writes=[t_S])
    P.dma(DMA(S2[4:48, :], I["conv_b"].ap().rearrange("(c p) -> c p", p=128)), writes=[t_S])
    P.pe(TR(PS[0][:, 0:128], S1, identf4), reads=[t_S, t_identf], writes=[tps[0]])
    P.pe(TR(PS[0][:, 128:176], S2[0:48, :], identf4[0:48, 0:48]), reads=[t_S, t_identf], writes=[tps[0]])
    P.act(ACT(cwT, PS[0][:, 0:176], AF.Copy), reads=[tps[0]], writes=[t_cw])
    P.pool(MEMSET(halo, 0.0), writes=t_halo)

    NST = T // 256
    for stl in range(NST):
        hp = stl % 2
        for i in range(2):
            tt0 = stl * 256 + i * 128
            xf = XF[(stl % 2) * 2 + i]; t_xf = t_XF[(stl % 2) * 2 + i]
            P.dma(DMA(xf, x1s.ap()[tt0:tt0 + 128, :]), writes=[t_xf])
            P.act(ACT(hb4[i], xf, AF.Square, accum_out=sm4[:, 0:1]), reads=[t_xf], writes=[t_hb4[i], t_sm4])
            P.dve(TS(sm4[:, 1:2], sm4[:, 0:1], 1.0 / D, EPS, ALU.mult, ALU.add), reads=[t_sm4], writes=[t_sm4])
            P.pool(TT(sm4[:, 2:3], sm4[:, 1:2], nhalf4[:, 0:1], ALU.pow), reads=[t_sm4, t_nhalf], writes=[t_sm4])
            P.dve(STT(hb4[i], xf, sm4[:, 2:3], gF, ALU.mult, ALU.mult), reads=[t_xf, t_sm4, t_gF], writes=[t_hb4[i]])
            for k in range(8):
                P.pe(TR(PSB[6 + i][:, k * 128:(k + 1) * 128], hb4[i][:, k * 128:(k + 1) * 128], identb4),
                     reads=[t_hb4[i], t_identb], writes=[tps[6 + i]])
            P.act(ACT(h2T[hp][:, :, i * 128:(i + 1) * 128], PSB[6 + i][:, :].rearrange("p (k t) -> p k t", k=8), AF.Copy),
                  reads=[tps[6 + i]], writes=[t_h2T[hp]])
        for c in range(NFC):
            bank = 4 + (c % 2)
            for half in range(2):
                fc = half * NFC + c
                for k in range(8):
                    P.pe(MM(PS[bank][:, half * 256:(half + 1) * 256], Wup[:, k, fc * 128:(fc + 1) * 128], h2T[hp][:, k, :],
                            k == 0, k == 7), reads=[t_Wup, t_h2T[hp]], writes=[tps[bank]])
            ai = c % 2
            for half in range(2):
                fc = half * NFC + c
                src = PS[bank][:, half * 256:(half + 1) * 256]
                ui = (2 * c + half) % 4
                Ubuf = Ub[ui]; t_u = t_Ub[ui]
                P.pool(CP(Ubuf[:, 0:2], halo[:, fc, :]), reads=[t_halo[fc]], writes=[t_u])
                P.act(ACT(Ubuf[:, 2:258], src, AF.Copy), reads=[tps[bank]], writes=[t_u])
                P.pool(CP(halo[:, fc, :], Ubuf[:, 256:258]), reads=[t_u], writes=[t_halo[fc]])
                P.act(ACT(acc[ai][:, half, :], src, AF.Identity, scale=cw[:, 2, fc:fc + 1], bias=cbias[:, fc:fc + 1]),
                      reads=[tps[bank], t_cw, t_cb], writes=[t_acc[ai]])
                eng = P.dve
                eng(STT(acc[ai][:, half, :], Ubuf[:, 1:257], cw[:, 1, fc:fc + 1], acc[ai][:, half, :], ALU.mult, ALU.add),
                    reads=[t_u, t_cw, t_acc[ai]], writes=[t_acc[ai]])
                eng(STT(acc[ai][:, half, :], Ubuf[:, 0:256], cw[:, 0, fc:fc + 1], acc[ai][:, half, :], ALU.mult, ALU.add),
                    reads=[t_u, t_cw, t_acc[ai]], writes=[t_acc[ai]])
            P.act(ACT(sg[ai], acc[ai][:, 1, :], AF.Silu), reads=[t_acc[ai]], writes=[t_sg[ai]])
            a3 = c % 3
            P.dve(TT(actT[a3], sg[ai], acc[ai][:, 0, :], ALU.mult), reads=[t_sg[ai], t_acc[ai]], writes=[t_actT[a3]])
            for i in range(2):
                for nh in range(2):
                    bk = i * 2 + nh
                    P.pe(MM(PS[bk][:, :], actT[a3][:, i * 128:(i + 1) * 128], Wdn[:, c, nh * 512:(nh + 1) * 512],
                            c == 0, c == NFC - 1), reads=[t_actT[a3], t_Wdn], writes=[tps[bk]])
        for i in range(2):
            tt0 = stl * 256 + i * 128
            xf = XF[(stl % 2) * 2 + i]; t_xf = t_XF[(stl % 2) * 2 + i]
            ot = OT[i]
            for nh in range(2):
                bk = i * 2 + nh
                P.dve(TT(ot[:, nh * 512:(nh + 1) * 512], PS[bk][:, :], xf[:, nh * 512:(nh + 1) * 512], ALU.add),
                      reads=[tps[bk], t_xf], writes=[t_OT[i]])
            P.dma(DMA(out.ap()[tt0:tt0 + 128, :], ot), reads=[t_OT[i]])
    P.emit()
    return nc, P


_CACHE = {}


def kernel(**inputs):
    if "nc" not in _CACHE:
        _CACHE["nc"] = build_nc()[0]
        _CACHE["consts"] = host_consts()
    nc = _CACHE["nc"]
    consts = _CACHE["consts"]
    shared = {}
    for k in IN_SHAPES:
        if k == "x":
            continue
        a = np.asarray(inputs[k], dtype=np.float32)
        shared[k] = np.ascontiguousarray(a.reshape(IN_SHAPES[k]))
    shared.update(consts)
    x = np.asarray(inputs["x"], dtype=np.float32)
    in_maps = []
    for b in range(8):
        m = dict(shared)
        m["x"] = np.ascontiguousarray(x[b])
        in_maps.append(m)
    res = run_bass_kernel_spmd(nc, in_maps, core_ids=list(range(8)))
    return np.stack([np.asarray(r["out"], dtype=np.float32) for r in res.results], axis=0)
```

```python
import contextlib
import os
import numpy as np
import concourse.bass as bass
import concourse.mybir as mybir
from concourse.bass_utils import run_bass_kernel_spmd

F32 = mybir.dt.float32
BF16 = mybir.dt.bfloat16
AF = mybir.ActivationFunctionType
ALU = mybir.AluOpType
AX = mybir.AxisListType

T = 4096
D = 1024
NT = T // 128
DFF = 2816
NFC = DFF // 128
EPS = 1e-6
NEG = -30000.0
DF = 384

ENGS = ("pe", "act", "dve", "pool", "sp")
PH_FREE_DEFAULT = "1,4"


class Tok:
    __slots__ = ("w", "r", "x")

    def __init__(self, x=False):
        self.w = None
        self.r = []
        self.x = x


class Prog:
    NDMA = 12

    def __init__(self, nc):
        self.nc = nc
        self.ops = []
        self.last = {e: None for e in ENGS}
        self.dmas = []

    def op(self, eng, fn, reads=(), writes=(), dma=False, nss=False, extra=()):
        import os
        if fn is not None and len(self.ops) >= int(os.environ.get("MAXOPS", "100000000")):
            return -1
        oid = len(self.ops)
        deps = set(extra)
        if any(t.x for t in reads):
            writes = list(writes) + [t for t in reads if t.x]
            reads = [t for t in reads if not t.x]
        smode = getattr(self, "mode", None) or os.environ.get("SERIAL", "2")
        if smode == "1" or (smode in ("2", "s2pe") and not dma) or (smode == "3"):
            chain = (smode != "3") or dma or getattr(self, "prev_was_dma", False)
            pe_run = (smode == "s2pe" and eng == "pe" and fn is not None and getattr(self, "prev_eng", None) == "pe")
            if pe_run:
                deps.update(getattr(self, "pe_run_deps", ()))
            elif chain and getattr(self, "prev_real", None) is not None:
                deps.add(self.prev_real)
                if smode == "s2pe" and eng == "pe":
                    self.pe_run_deps = [self.prev_real]
            if smode != "3" and not (smode == "s2pe" and eng == "pe"):
                nss = False
            if fn is not None:
                self.prev_real = oid
                self.prev_was_dma = dma
                self.prev_eng = eng
        if smode in ("r1", "r2") and not dma and fn is not None:
            cls = eng if smode == "r1" else ("P" if eng == "pe" else "E")
            if cls != getattr(self, "cur_cls", None):
                self.run_deps = [v for v in self.last.values() if v is not None]
                self.cur_cls = cls
            deps.update(self.run_deps)
        if smode == "self" and not dma and fn is not None:
            nss = False
            if self.last[eng] is not None:
                deps.add(self.last[eng])
        if smode == "4":
            nss = False
        if smode.startswith("c:") and not dma and eng in smode[2:].split(",") and int(os.environ.get("CH_LO", "0")) <= oid < int(os.environ.get("CH_HI", "100000000")):
            nss = False
            if getattr(self, "prev_c", None) is not None:
                deps.add(self.prev_c)
            if fn is not None:
                self.prev_c = oid
        if smode == "5" and eng in ("act", "dve", "pool") and not dma:
            if getattr(self, "prev_ew", None) is not None:
                deps.add(self.prev_ew)
            if fn is not None:
                self.prev_ew = oid
        for t in reads:
            if t.w is not None:
                deps.add(t.w)
        for t in writes:
            if t.w is not None:
                deps.add(t.w)
            deps.update(t.r)
        deps.discard(oid)
        if fn is not None and not dma:
            if eng == "pool" and self.last["dve"] is not None:
                deps.add(self.last["dve"])
            if eng == "dve" and self.last["pool"] is not None:
                deps.add(self.last["pool"])
        self.ops.append(dict(eng=eng, fn=fn, deps=deps, dma=dma, nss=nss))
        for t in reads:
            t.r.append(oid)
        for t in writes:
            t.w = oid
            t.r = []
        if dma:
            self.dmas.append(oid)
        else:
            if fn is not None:
                self.last[eng] = oid
        return oid

    def pe(self, fn, reads=(), writes=()):
        return self.op("pe", fn, reads, writes, nss=True)

    def act(self, fn, reads=(), writes=()):
        return self.op("act", fn, reads, writes)

    def dve(self, fn, reads=(), writes=()):
        return self.op("dve", fn, reads, writes)

    def pool(self, fn, reads=(), writes=()):
        return self.op("pool", fn, reads, writes)

    def dma(self, fn, reads=(), writes=(), q="sp"):
        return self.op(q, fn, reads, writes, dma=True)

    def set_phase(self, ph):
        free = os.environ.get("PH_FREE", PH_FREE_DEFAULT).split(",")
        self.mode = "0" if str(ph) in free else "2"

    def barrier(self):
        deps = [v for v in self.last.values() if v is not None] + list(self.dmas)
        self.dmas = []
        for e in ENGS:
            self.op(e, None, extra=deps)

    def emit(self):
        nc = self.nc
        ops = self.ops
        n = len(ops)

        def skip(o, od):
            return od["eng"] == o["eng"] and not o["dma"] and o["nss"] and o["fn"] is not None

        needed = [False] * n
        for o in ops:
            for d in o["deps"]:
                od = ops[d]
                if od["dma"] or skip(o, od):
                    continue
                needed[d] = True
        cnt = {e: 0 for e in ENGS}
        signo = [0] * n
        dma_cnt = {"sp": 0, "pool": 0}
        dma_slot = [None] * n
        for i, o in enumerate(ops):
            if o["dma"]:
                di = dma_cnt[o["eng"]]
                dma_slot[i] = ((o["eng"], di % self.NDMA), 16 * (di // self.NDMA + 1))
                dma_cnt[o["eng"]] += 1
            elif needed[i]:
                cnt[o["eng"]] += 1
                signo[i] = cnt[o["eng"]]
        self.stats = dict(cnt=dict(cnt), ndma=dict(dma_cnt), nops=n)
        es = contextlib.ExitStack()
        with es:
            sems = {e: es.enter_context(nc.semaphore("s_" + e)) for e in ENGS}
            dsems = {(q, k): es.enter_context(nc.semaphore("d%s_%d" % (q, k))) for q in ("sp", "pool") for k in range(self.NDMA)}
            block = es.enter_context(nc.Block())
            per_eng = {e: [] for e in ENGS}
            seen = {e: {} for e in ENGS}
            for i, o in enumerate(ops):
                e = o["eng"]
                waits = []

                def need(key, semh, val):
                    if seen[e].get(key, 0) >= val:
                        return
                    seen[e][key] = val
                    waits.append((semh, val))

                for d in sorted(o["deps"]):
                    od = ops[d]
                    if od["dma"]:
                        k, v = dma_slot[d]
                        need(("d", k), dsems[k], v)
                    else:
                        if skip(o, od):
                            continue
                        need(("e", od["eng"]), sems[od["eng"]], signo[d])
                inc = None
                if o["dma"]:
                    k, v = dma_slot[i]
                    if v > 16:
                        need(("d", k), dsems[k], v - 16)
                    inc = (dsems[k], 16)
                elif needed[i]:
                    inc = (sems[e], 1)
                per_eng[e].append((waits, o["fn"], inc))
            tail = []
            for q in ("sp", "pool"):
                for k in range(self.NDMA):
                    if dma_cnt[q] > k:
                        tail.append((dsems[(q, k)], 16 * ((dma_cnt[q] - 1 - k) // self.NDMA + 1)))
            tail += [(sems[e], cnt[e]) for e in ENGS if cnt[e] > 0]

            def runner(ename):
                def body(eng):
                    for waits, fn, inc in per_eng[ename]:
                        for semh, val in waits:
                            eng.wait_ge(semh, val)
                        if fn is None:
                            assert inc is None
                            continue
                        ins = fn(eng)
                        if inc is not None:
                            ins.then_inc(inc[0], inc[1])
                    if ename == "sp":
                        for semh, val in tail:
                            eng.wait_ge(semh, val)
                return body

            block.tensor(runner("pe"))
            block.scalar(runner("act"))
            block.vector(runner("dve"))
            block.gpsimd(runner("pool"))
            block.sync(runner("sp"))


def MM(out, lhsT, rhs, start=True, stop=True):
    return lambda e: e.matmul(out, lhsT=lhsT, rhs=rhs, start=start, stop=stop)


def TR(out, in_, ident):
    return lambda e: e.transpose(out=out, in_=in_, identity=ident)


def ACT(out, in_, func, **kw):
    return lambda e: e.activation(out=out, in_=in_, func=func, **kw)


def TT(out, a, b, op):
    return lambda e: e.tensor_tensor(out=out, in0=a, in1=b, op=op)


def TS(out, a, s1, s2, op0, op1=None):
    if op1 is None:
        return lambda e: e.tensor_scalar(out=out, in0=a, scalar1=s1, scalar2=None, op0=op0)
    return lambda e: e.tensor_scalar(out=out, in0=a, scalar1=s1, scalar2=s2, op0=op0, op1=op1)


def STT(out, a, s, b, op0, op1):
    return lambda e: e.scalar_tensor_tensor(out=out, in0=a, scalar=s, in1=b, op0=op0, op1=op1)


def CP(out, in_):
    return lambda e: e.tensor_copy(out=out, in_=in_)


def DMA(out, in_, **kw):
    return lambda e: e.dma_start(out=out, in_=in_, **kw)


def DMAN(out, in_):
    return lambda e: e.dma_start(out=out, in_=in_, allow_slow_non_contiguous=True)


def MEMSET(ap, v):
    return lambda e: e.memset(ap, v)


def RSUM(out, in_):
    return lambda e: e.reduce_sum(out=out, in_=in_, axis=AX.X)


class Arena:
    def __init__(self, nc, words):
        self.t = nc.alloc_sbuf_tensor("arena", [128, words], F32)
        self.words = words
        self.off = 0

    def alloc(self, free, dt):
        n = int(np.prod(free))
        w = n if dt == F32 else (n + 1) // 2
        w = (w + 7) // 8 * 8
        assert self.off + w <= self.words, ("SBUF arena overflow", self.off, w, self.words)
        v = self.t[:, self.off:self.off + w]
        self.off += w
        if dt != F32:
            v = v.bitcast(dt)
        v = v[:, 0:n]
        if len(free) == 2:
            v = v.rearrange("p (a b) -> p a b", a=free[0])
        elif len(free) == 3:
            v = v.rearrange("p (a b c) -> p a b c", a=free[0], b=free[1])
        return v


def _bucket_np(dist):
    dist = np.asarray(dist, dtype=np.int64)
    d = np.maximum(dist, 1).astype(np.float32)
    lg = (np.log(d / np.float32(16)) / np.float32(np.log(128 / 16)) * np.float32(16)).astype(np.float32)
    log_b = 16 + lg.astype(np.int32)
    log_b = np.clip(log_b, 16, 31)
    return np.where(dist < 16, np.maximum(dist, 0), log_b)


def host_consts():
    c = {}
    c["c_ident"] = np.eye(128, dtype=np.float32)
    c["c_anti"] = np.eye(128, dtype=np.float32)[::-1].copy()
    ohd = np.zeros((33, DF), np.float32)
    for idx in range(DF):
        d = idx - 128
        if d < 0:
            ohd[32, idx] = NEG
        else:
            ohd[_bucket_np(d), idx] += 1.0
            ohd[31, idx] -= 1.0
    c["c_ohd"] = ohd
    eb = np.zeros((64, T), np.float32)
    for s in range(64):
        eb[s, s * 64:(s + 1) * 64] = 1.0
    c["c_eband"] = eb
    db = np.zeros((16, 376), np.float32)
    for k in range(16):
        db[k, k + 239] = 1.0
    c["c_dband"] = db
    ov = np.zeros((256, 64), np.float32)
    for n in range(255):
        for s in range(64):
            if 16 * n < 64 * s + 64 and 16 * n + 32 > 64 * s:
                ov[n, s] = 1.0
    c["c_ov"] = ov
    ar = np.zeros((128, 127), np.float32)
    for q in range(128):
        hi = 1 if q >= 64 else 0
        for j in range(127):
            sp = j - 63
            if sp > hi:
                ar[q, j] = -1e30
            elif sp == hi or sp == hi - 1:
                ar[q, j] = 8.0
    c["c_arel"] = ar
    ed = np.full((128, 128), NEG, np.float32)
    for k in range(128):
        ed[k, :k] = 0.0
    c["c_edge"] = ed
    c["c_tril"] = np.tril(np.ones((128, 128), np.float32))
    return c


CONST_SHAPES = dict(c_ident=[128, 128], c_anti=[128, 128], c_ohd=[33, DF], c_eband=[64, T], c_dband=[16, 376],
                    c_ov=[256, 64], c_arel=[128, 127], c_edge=[128, 128], c_tril=[128, 128])

IN_SHAPES = dict(
    x=[T, D], rel_bias=[32, 8], attn_norm=[D], w_in=[D, 2328], q_norm=[64], k_norm_cmp=[64], k_norm_slc=[64],
    k_norm_win=[64], cmp_pe_k=[32, 64], cmp_w1_k=[2048, 256], cmp_b1_k=[256], cmp_w2_k=[256, 64],
    cmp_pe_v=[32, 64], cmp_w1_v=[2048, 256], cmp_b1_v=[256], cmp_w2_v=[256, 64], sgu_norm=[512],
    sgu_w=[8, 128, 128], sgu_b=[8, 128], w_out=[D, D], ffn_norm=[D], w_up=[D, 2 * DFF], conv_w=[3, 2 * DFF],
    conv_b=[2 * DFF], w_down=[DFF, D])


def build_nc(dbg=False, stop_after=4, p1_tiles=NT):
    nc = bass.Bass("TRN2", target_bir_lowering=False)
    I = {k: nc.dram_tensor(k, s, F32, kind="ExternalInput") for k, s in IN_SHAPES.items()}
    C = {k: nc.dram_tensor(k, s, F32, kind="ExternalInput") for k, s in CONST_SHAPES.items()}
    out = nc.dram_tensor("out", [T, D], F32, kind="ExternalOutput")
    x1s = nc.dram_tensor("x1s", [T, D], F32, kind="ExternalOutput" if dbg else "Internal")
    fscr = nc.dram_tensor("fscr", [8, DF], F32, kind="Internal")
    dmix = nc.dram_tensor("dmix", [T, D], BF16, kind="ExternalOutput") if dbg else None
    dkc = nc.dram_tensor("dkc", [128, 256], BF16, kind="ExternalOutput") if dbg else None
    dvc = nc.dram_tensor("dvc", [128, 260], BF16, kind="ExternalOutput") if dbg else None

    P = Prog(nc)
    A = Arena(nc, 52000)
    PS = [nc.alloc_psum_tensor("ps%d" % i, [128, 512], F32) for i in range(8)]
    PSB = [p.bitcast(BF16) for p in PS]
    tps = [Tok(x=True) for _ in range(8)]

    def bcast_dram(t, n):
        return bass.AP(t, 0, [[0, 128], [1, n]])

    identb = A.alloc([128], BF16); t_identb = Tok()
    identf = A.alloc([128], F32); t_identf = Tok()
    nhalf = A.alloc([16], F32); t_nhalf = Tok()
    QT = A.alloc([4, T], BF16); t_QT = [Tok() for _ in range(NT)]
    KST = A.alloc([T], BF16); t_KST = [Tok() for _ in range(NT)]
    KWT = A.alloc([T], BF16); t_KWT = [Tok() for _ in range(NT)]
    KCc = A.alloc([T], BF16); t_KCc = [Tok() for _ in range(NT)]
    VCc = A.alloc([T], BF16); t_VCc = [Tok() for _ in range(NT)]
    Vs = A.alloc([NT, 2, 65], BF16); t_Vs = [Tok() for _ in range(NT)]
    Vw = A.alloc([NT, 2, 65], BF16); t_Vw = [Tok() for _ in range(NT)]
    gates = A.alloc([NT, 24], F32); t_gates = [Tok() for _ in range(NT)]
    OSGT = A.alloc([4, T], BF16); t_OSGT = [Tok() for _ in range(NT)]
    KCT = A.alloc([256], BF16); t_KCT = Tok()
    Vc = A.alloc([2, 2, 65], BF16); t_Vc = Tok()
    gfc = A.alloc([64], F32); t_gfc = Tok()
    GF4 = A.alloc([4, 64], F32); t_GF4 = Tok()
    persist_mark = A.off

    P.dma(DMA(identf, C["c_ident"].ap()), writes=[t_identf])
    P.dma(DMA(identb, C["c_ident"].ap()), writes=[t_identb], q="pool")
    P.pool(MEMSET(nhalf, -0.5), writes=[t_nhalf])
    t_ones = Tok()
    P.pool(MEMSET(Vs[:, :, :, 64:65], 1.0), writes=[t_ones])
    P.pool(MEMSET(Vw[:, :, :, 64:65], 1.0), writes=[t_ones])
    P.pool(MEMSET(Vc, 0.0), writes=[t_Vc])
    P.pool(MEMSET(Vc[:, :, :, 64:65], 1.0), writes=[t_Vc])
    P.pool(MEMSET(KCT, 0.0), writes=[t_KCT])
    gq = A.alloc([64], F32); gk = A.alloc([3, 64], F32); t_g = Tok()
    P.dma(DMA(gq, bcast_dram(I["q_norm"], 64)), writes=[t_g])
    for i, nm in enumerate(("k_norm_cmp", "k_norm_slc", "k_norm_win")):
        P.dma(DMA(gk[:, i, :], bcast_dram(I[nm], 64)), writes=[t_g])
    P.dve(STT(gfc, gk[:, 0, :], 0.125, gq, ALU.mult, ALU.mult), reads=[t_g], writes=[t_gfc])
    for j, i in enumerate((1, 1, 2, 2)):
        P.dve(STT(GF4[:, j, :], gk[:, i, :], 0.125, gq, ALU.mult, ALU.mult), reads=[t_g], writes=[t_GF4])

    def rsqrt_mean(ss, n_out, nmean, tmp, out, t_in, t_out):
        P.dve(TS(tmp, ss, 1.0 / nmean, EPS, ALU.mult, ALU.add), reads=t_in, writes=t_out)
        P.pool(TT(out, tmp, nhalf[:, 0:n_out], ALU.pow), reads=t_out + [t_nhalf], writes=t_out)

    P.set_phase(1)
    ph_mark = A.off
    Win = A.alloc([8, 2328], BF16); t_Win = Tok()
    gA = A.alloc([D], F32); t_gA = Tok()
    gSG = A.alloc([512], F32); t_gSG = Tok()
    WT = A.alloc([8, 128], BF16); t_WT = Tok()
    sbT = A.alloc([8], F32); t_sbT = Tok()
    XT = [A.alloc([D], F32) for _ in range(2)]; t_XT = [Tok(), Tok()]
    hb = A.alloc([D], BF16); t_hb = Tok()
    hT = A.alloc([8, 128], BF16); t_hT = Tok()
    qf = A.alloc([512], F32); t_qf = Tok()
    sq = A.alloc([512], F32); t_sq = Tok()
    kf = A.alloc([256], F32); t_kf = Tok()
    qb16 = A.alloc([512], BF16); t_qb16 = Tok()
    kb16 = A.alloc([256], BF16); t_kb16 = Tok()
    cb16 = A.alloc([256], BF16); t_cb16 = Tok()
    uf = A.alloc([512], F32); t_uf = Tok()
    vf = A.alloc([512], F32); t_vf = Tok()
    vb = A.alloc([512], BF16); t_vb = Tok()
    zf = sq; t_zf = t_sq
    osg = A.alloc([512], BF16); t_osg = Tok()
    sm = A.alloc([64], F32); t_sm = Tok()
    gt = A.alloc([24], F32); t_gt = Tok()
    sgw = A.alloc([128], F32); t_sgw = Tok()
    tril = A.alloc([128], F32); t_tril = Tok()

    w_in_v = I["w_in"].ap().rearrange("(k p) n -> p k n", p=128)
    col = 0
    segs = []
    for r in range(4):
        for g in range(2):
            segs.append(((g * 4 + r) * 64, 64))
    segs += [(768, 128), (1024, 128), (512, 128), (640, 128)]
    segs += [(896, 128), (1152, 128), (1280, 24)]
    segs += [(1304, 1024)]
    for (c0, n) in segs:
        P.dma(DMA(Win[:, :, col:col + n], w_in_v[:, :, c0:c0 + n]), writes=[t_Win], q="pool")
        col += n
    assert col == 2328
    P.dma(DMA(gA, bcast_dram(I["attn_norm"], D)), writes=[t_gA])
    P.dma(DMA(gSG, bcast_dram(I["sgu_norm"], 512)), writes=[t_gSG])
    P.dma(DMA(tril, C["c_tril"].ap()), writes=[t_tril])
    P.dma(DMAN(sbT, I["sgu_b"].ap().rearrange("g i -> i g")), writes=[t_sbT])
    for g in range(8):
        P.dma(DMA(sgw, I["sgu_w"].ap()[g]), writes=[t_sgw])
        P.dve(TT(sgw, sgw, tril, ALU.mult), reads=[t_sgw, t_tril], writes=[t_sgw])
        P.pe(TR(PS[7][:, 0:128], sgw, identf), reads=[t_sgw, t_identf], writes=[tps[7]])
        P.act(ACT(WT[:, g, :], PS[7][:, 0:128], AF.Copy), reads=[tps[7]], writes=[t_WT])

    CH = [(0, 512), (512, 512), (1024, 280), (1304, 512), (1816, 512)]
    print("ops before tiles", len(P.ops))
    for qb in range(p1_tiles):
        t0 = qb * 128
        print("tile", qb, "starts at op", len(P.ops))
        xt = XT[qb % 2]; t_xt = t_XT[qb % 2]
        P.dma(DMA(xt, I["x"].ap()[t0:t0 + 128, :]), writes=[t_xt])
        P.act(ACT(hb, xt, AF.Square, accum_out=sm[:, 0:1]), reads=[t_xt], writes=[t_hb, t_sm])
        rsqrt_mean(sm[:, 0:1], 1, D, sm[:, 1:2], sm[:, 2:3], [t_sm], [t_sm])
        P.dve(STT(hb, xt, sm[:, 2:3], gA, ALU.mult, ALU.mult), reads=[t_xt, t_sm, t_gA], writes=[t_hb])
        for k in range(8):
            P.pe(TR(PSB[0][:, k * 128:(k + 1) * 128], hb[:, k * 128:(k + 1) * 128], identb),
                 reads=[t_hb, t_identb], writes=[tps[0]])
        P.act(ACT(hT, PSB[0][:, :].rearrange("p (k t) -> p k t", k=8), AF.Copy), reads=[tps[0]], writes=[t_hT])
        for ci, (c0, n) in enumerate(CH):
            for k in range(8):
                P.pe(MM(PS[1 + ci][:, 0:n], hT[:, k, :], Win[:, k, c0:c0 + n], k == 0, k == 7),
                     reads=[t_hT, t_Win], writes=[tps[1 + ci]])
        P.act(ACT(qf, PS[1][:, :], AF.Copy), reads=[tps[1]], writes=[t_qf])
        P.dve(TT(sq, qf, qf, ALU.mult), reads=[t_qf], writes=[t_sq])
        P.dve(RSUM(sm[:, 8:16], sq.rearrange("p (h d) -> p h d", h=8)), reads=[t_sq], writes=[t_sm])
        P.act(ACT(kf, PS[2][:, 0:256], AF.Copy), reads=[tps[2]], writes=[t_kf])
        P.dve(TT(sq[:, 0:256], kf, kf, ALU.mult), reads=[t_kf], writes=[t_sq])
        P.dve(RSUM(sm[:, 16:20], sq[:, 0:256].rearrange("p (h d) -> p h d", h=4)), reads=[t_sq], writes=[t_sm])
        rsqrt_mean(sm[:, 8:20], 12, 64, sm[:, 20:32], sm[:, 32:44], [t_sm], [t_sm])
        P.dve(TT(qb16.rearrange("p (h d) -> p h d", h=8), qf.rearrange("p (h d) -> p h d", h=8),
                 sm[:, 32:40].unsqueeze(2).to_broadcast([128, 8, 64]), ALU.mult), reads=[t_qf, t_sm], writes=[t_qb16])
        P.dve(TT(kf.rearrange("p (h d) -> p h d", h=4), kf.rearrange("p (h d) -> p h d", h=4),
                 sm[:, 40:44].unsqueeze(2).to_broadcast([128, 4, 64]), ALU.mult), reads=[t_kf, t_sm], writes=[t_kf])
        P.dve(TT(kb16, kf, GF4.rearrange("p a b -> p (a b)"), ALU.mult), reads=[t_kf, t_GF4], writes=[t_kb16])
        P.act(ACT(cb16, PS[2][:, 256:512], AF.Copy), reads=[tps[2]], writes=[t_cb16])
        for j in range(4):
            P.pe(TR(PSB[7][:, j * 128:(j + 1) * 128], qb16[:, j * 128:(j + 1) * 128], identb),
                 reads=[t_qb16, t_identb], writes=[tps[7]])
        for j in range(2):
            P.pe(TR(PSB[7][:, 512 + j * 128:640 + j * 128], kb16[:, j * 128:(j + 1) * 128], identb),
                 reads=[t_kb16, t_identb], writes=[tps[7]])
            P.pe(TR(PSB[7][:, 768 + j * 128:896 + j * 128], cb16[:, j * 128:(j + 1) * 128], identb),
                 reads=[t_cb16, t_identb], writes=[tps[7]])
        P.act(ACT(QT[:, :, t0:t0 + 128], PSB[7][:, 0:512].rearrange("p (j t) -> p j t", j=4), AF.Copy),
              reads=[tps[7]], writes=[t_QT[qb]])
        P.dve(CP(KST[:, t0:t0 + 128], PSB[7][:, 512:640]), reads=[tps[7]], writes=[t_KST[qb]])
        P.dve(CP(KWT[:, t0:t0 + 128], PSB[7][:, 640:768]), reads=[tps[7]], writes=[t_KWT[qb]])
        P.act(ACT(KCc[:, t0:t0 + 128], PSB[7][:, 768:896], AF.Copy), reads=[tps[7]], writes=[t_KCc[qb]])
        P.act(ACT(VCc[:, t0:t0 + 128], PSB[7][:, 896:1024], AF.Copy), reads=[tps[7]], writes=[t_VCc[qb]])
        P.act(ACT(Vs[:, qb, :, 0:64], PS[3][:, 0:128].rearrange("p (g d) -> p g d", g=2), AF.Copy),
              reads=[tps[3], t_ones], writes=[t_Vs[qb]])
        P.act(ACT(Vw[:, qb, :, 0:64], PS[3][:, 128:256].rearrange("p (g d) -> p g d", g=2), AF.Copy),
              reads=[tps[3], t_ones], writes=[t_Vw[qb]])
        P.act(ACT(gt, PS[3][:, 256:280], AF.Tanh, scale=0.5), reads=[tps[3]], writes=[t_gt])
        P.dve(TS(gates[:, qb, :], gt, 0.5, 0.5, ALU.mult, ALU.add), reads=[t_gt], writes=[t_gates[qb]])
        P.act(ACT(uf, PS[4][:, :], AF.Gelu_apprx_tanh), reads=[tps[4]], writes=[t_uf])
        P.act(ACT(vf, PS[5][:, :], AF.Gelu_apprx_tanh), reads=[tps[5]], writes=[t_vf])
        P.act(ACT(zf, vf, AF.Square, accum_out=sm[:, 48:49]), reads=[t_vf], writes=[t_zf, t_sm])
        rsqrt_mean(sm[:, 48:49], 1, 512, sm[:, 49:50], sm[:, 50:51], [t_sm], [t_sm])
        P.dve(STT(vb, vf, sm[:, 50:51], gSG, ALU.mult, ALU.mult), reads=[t_vf, t_sm, t_gSG], writes=[t_vb])
        for g in range(8):
            P.pe(MM(PS[6][:, g * 64:(g + 1) * 64], WT[:, g, :], vb[:, g * 64:(g + 1) * 64]),
                 reads=[t_WT, t_vb], writes=[tps[6]])
        P.dve(TT(zf.rearrange("p (g e) -> p g e", g=8), PS[6][:, :].rearrange("p (g e) -> p g e", g=8),
                 sbT.unsqueeze(2).to_broadcast([128, 8, 64]), ALU.add), reads=[tps[6], t_sbT], writes=[t_zf])
        P.dve(TT(osg, zf, uf, ALU.mult), reads=[t_zf, t_uf], writes=[t_osg])
        for j in range(4):
            P.pe(TR(PSB[0][:, j * 128:(j + 1) * 128], osg[:, j * 128:(j + 1) * 128], identb),
                 reads=[t_osg, t_identb], writes=[tps[0]])
        P.act(ACT(OSGT[:, :, t0:t0 + 128], PSB[0][:, 0:512].rearrange("p (j t) -> p j t", j=4), AF.Copy),
              reads=[tps[0]], writes=[t_OSGT[qb]])
        if dbg:
            P.dma(DMA(dmix[t0:t0 + 128, 512:1024], osg), reads=[t_osg])

    P.barrier()
    if stop_after == 1:
        P.emit()
        return nc, P

    P.set_phase(2)
    A.off = ph_mark
    W1 = A.alloc([32, 256], BF16); t_W1 = Tok()
    W2 = A.alloc([2, 64], BF16); t_W2 = Tok()
    peT = A.alloc([32], BF16); t_peT = Tok()
    b1 = A.alloc([2], F32); t_b1 = Tok()
    cst = A.alloc([2], F32); t_cst = Tok()
    hidT = A.alloc([2, 256], BF16); t_hid = Tok()
    kcf = A.alloc([2, 2, 64], F32); t_kcf = Tok()
    kcs = A.alloc([2, 2, 64], F32); t_kcs = Tok()
    kcb = A.alloc([2, 128], BF16); t_kcb = Tok()
    sm2 = A.alloc([16], F32); t_sm2 = Tok()
    P.pool(MEMSET(hidT, 0.0), writes=[t_hid])
    P.pool(MEMSET(kcf, 0.0), writes=[t_kcf])
    for kv in ("k", "v"):
        src = KCc if kv == "k" else VCc
        t_src = t_KCc if kv == "k" else t_VCc
        w1v = I["cmp_w1_" + kv].ap().rearrange("(j d) h -> d j h", d=64)
        for half in range(2):
            P.dma(DMA(W1[half * 64:(half + 1) * 64, :, :], w1v), writes=[t_W1], q="pool")
        P.dma(DMA(W2, I["cmp_w2_" + kv].ap().rearrange("(h p) d -> p h d", p=128)), writes=[t_W2], q="pool")
        P.dma(DMAN(peT[0:64, :], I["cmp_pe_" + kv].ap().rearrange("j d -> d j")), writes=[t_peT], q="pool")
        P.dma(DMAN(b1, I["cmp_b1_" + kv].ap().rearrange("(h p) -> p h", p=128)), writes=[t_b1])
        for half in range(2):
            for j in range(32):
                P.pe(MM(PS[0][:, half:half + 1], W1[0:64, j, half * 128:(half + 1) * 128], peT[0:64, j:j + 1],
                        j == 0, j == 31), reads=[t_W1, t_peT], writes=[tps[0]])
        P.dve(TT(cst, PS[0][:, 0:2], b1, ALU.add), reads=[tps[0], t_b1], writes=[t_cst])
        for g in range(2):
            for half in range(2):
                bank = 1 + half
                for j in range(32):
                    P.pe(MM(PS[bank][:, 0:255], W1[g * 64:(g + 1) * 64, j, half * 128:(half + 1) * 128],
                            src[g * 64:(g + 1) * 64, j:j + 4065:16], j == 0, j == 31),
                         reads=[t_W1] + t_src, writes=[tps[bank]])
                P.act(ACT(hidT[:, half, 0:255], PS[bank][:, 0:255], AF.Gelu_apprx_tanh, bias=cst[:, half:half + 1]),
                      reads=[tps[bank], t_cst], writes=[t_hid])
            for nt in range(2):
                M = 128 if nt == 0 else 127
                for half in range(2):
                    P.pe(MM(PS[3 + nt][0:M, 0:64], hidT[:, half, nt * 128:nt * 128 + M], W2[:, half, :],
                            half == 0, half == 1), reads=[t_hid, t_W2], writes=[tps[3 + nt]])
                if kv == "v":
                    P.act(ACT(Vc[0:M, nt, g, 0:64], PS[3 + nt][0:M, 0:64], AF.Copy), reads=[tps[3 + nt]], writes=[t_Vc])
                else:
                    P.act(ACT(kcf[0:M, nt, g, :], PS[3 + nt][0:M, 0:64], AF.Copy), reads=[tps[3 + nt]], writes=[t_kcf])
        if kv == "k":
            P.dve(TT(kcs, kcf, kcf, ALU.mult), reads=[t_kcf], writes=[t_kcs])
            P.dve(RSUM(sm2[:, 0:4], kcs.rearrange("p a b c -> p (a b) c")), reads=[t_kcs], writes=[t_sm2])
            rsqrt_mean(sm2[:, 0:4], 4, 64, sm2[:, 4:8], sm2[:, 8:12], [t_sm2], [t_sm2])
            P.dve(TT(kcf.rearrange("p a b c -> p (a b) c"), kcf.rearrange("p a b c -> p (a b) c"),
                     sm2[:, 8:12].unsqueeze(2).to_broadcast([128, 4, 64]), ALU.mult), reads=[t_kcf, t_sm2], writes=[t_kcf])
            P.dve(TT(kcb.rearrange("p a (b c) -> p (a b) c", b=2), kcf.rearrange("p a b c -> p (a b) c"),
                     gfc.unsqueeze(1).to_broadcast([128, 4, 64]), ALU.mult), reads=[t_kcf, t_gfc], writes=[t_kcb])
            for nt in range(2):
                M = 128 if nt == 0 else 127
                P.pe(TR(PSB[5][:, nt * 128:nt * 128 + M], kcb[0:M, nt, :], identb[0:M, 0:M]),
                     reads=[t_kcb, t_identb], writes=[tps[5]])
            P.act(ACT(KCT[:, 0:255], PSB[5][:, 0:255], AF.Copy), reads=[tps[5]], writes=[t_KCT])
    if dbg:
        P.dma(DMA(dkc.ap(), KCT), reads=[t_KCT])
        P.dma(DMA(dvc.ap(), Vc.rearrange("p a b c -> p (a b c)")), reads=[t_Vc])
    P.barrier()
    if stop_after == 2:
        P.emit()
        return nc, P

    P.set_phase(3)
    A.off = ph_mark
    Wout = A.alloc([8, D], BF16); t_Wout = Tok()
    Eband = A.alloc([T], BF16); t_Eb = Tok()
    dband = A.alloc([376], BF16); t_db = Tok()
    ovb = A.alloc([2, 64], BF16); t_ov = Tok()
    arel = A.alloc([127], F32); t_arel = Tok()
    edge = A.alloc([128], BF16); t_edge = Tok()
    anti = A.alloc([128], F32); t_anti = Tok()
    rbx = A.alloc([8], F32); t_rbx = Tok()
    ohd = A.alloc([DF], F32); t_ohd = Tok()
    Fs = A.alloc([DF], F32); t_Fs = Tok()
    Hk = A.alloc([8, 128], F32); t_Hk = Tok()
    BTd = A.alloc([8, 128], BF16); BTo = A.alloc([8, 128], BF16); BTc = A.alloc([8, 128], BF16); t_BT = Tok()
    PTs = [A.alloc([512], BF16) for _ in range(4)]; t_PT = [Tok() for _ in range(4)]
    XT3 = [A.alloc([D], F32) for _ in range(2)]; t_XT3 = [Tok(), Tok()]
    X1 = [A.alloc([D], F32) for _ in range(2)]; t_X1 = [Tok(), Tok()]
    ONSA = A.alloc([512], BF16); t_ONSA = Tok()
    MIXT = A.alloc([4, 128], BF16); t_MIXT = Tok()
    NMT = A.alloc([128], BF16); t_NMT = Tok()
    negm = A.alloc([64], BF16); t_negm = Tok()
    e1 = A.alloc([4, 64], F32); t_e1 = Tok()
    e2 = A.alloc([4, 64], F32); t_e2 = Tok()
    score = A.alloc([64], F32); t_score = Tok()
    wk = A.alloc([64], F32); t_wk = Tok()
    m8 = A.alloc([16], F32); t_m8 = Tok()
    st = A.alloc([32], F32); t_st = Tok()

    for c0 in range(0, 8, 2):
        P.dma(DMA(Wout[:, c0:c0 + 2, :], I["w_out"].ap().rearrange("(k p) n -> p k n", p=128)[:, c0:c0 + 2, :]),
              writes=[t_Wout], q="pool")
    P.dma(DMA(Eband[0:64, :], C["c_eband"].ap()), writes=[t_Eb], q="pool")
    P.dma(DMA(dband[0:16, :], C["c_dband"].ap()), writes=[t_db], q="pool")
    P.dma(DMA(ovb, C["c_ov"].ap().rearrange("(t p) s -> p t s", p=128)), writes=[t_ov], q="pool")
    P.dma(DMA(arel, C["c_arel"].ap()), writes=[t_arel])
    P.dma(DMA(edge, C["c_edge"].ap()), writes=[t_edge], q="pool")
    P.dma(DMA(anti, C["c_anti"].ap()), writes=[t_anti])
    P.pool(MEMSET(rbx[32:33, :], 1.0), writes=[t_rbx])
    P.dma(DMA(rbx[0:32, :], I["rel_bias"].ap()), writes=[t_rbx])
    P.dma(DMA(ohd[0:33, :], C["c_ohd"].ap()), writes=[t_ohd])
    P.pe(MM(PS[0][0:8, 0:DF], rbx[0:33, 0:8], ohd[0:33, :]), reads=[t_rbx, t_ohd], writes=[tps[0]])
    P.act(ACT(Fs[0:8, :], PS[0][0:8, 0:DF], AF.Copy), reads=[tps[0]], writes=[t_Fs])
    P.dma(DMA(fscr.ap(), Fs[0:8, :]), reads=[t_Fs], writes=[t_Fs])
    for (c0, BT, rows, pstep) in ((1, BTd, 128, 1), (129, BTo, 128, 1), (1, BTc, 16, 16)):
        P.dma(DMA(Hk[0:rows, :, :], bass.AP(fscr, c0, [[pstep, rows], [DF, 8], [1, 128]])), reads=[t_Fs], writes=[t_Hk])
        for hh in range(2):
            lhsT = anti if rows == 128 else anti[0:16, 112:128]
            P.pe(MM(PS[1 + hh][0:rows, :], lhsT, Hk[0:rows, hh * 4:(hh + 1) * 4, :]), reads=[t_anti, t_Hk],
                 writes=[tps[1 + hh]])
            P.act(ACT(BT[0:rows, hh * 4:(hh + 1) * 4, :], PS[1 + hh][0:rows, :].rearrange("p (h q) -> p h q", h=4),
                      AF.Copy), reads=[tps[1 + hh]], writes=[t_BT])

    sbank = [0, 1, 2]
    sctr = [0]
    pctr = [0]

    def score_exp(mms, Mv, reads):
        b = sbank[sctr[0] % 3]; sctr[0] += 1
        pi = pctr[0] % 4; pctr[0] += 1
        for i, (lhsT, rhs) in enumerate(mms):
            P.pe(MM(PS[b][0:Mv, :].rearrange("p (r q) -> p r q", r=4), lhsT, rhs, i == 0, i == len(mms) - 1),
                 reads=reads, writes=[tps[b]])
        P.act(ACT(PTs[pi][0:Mv, :], PS[b][0:Mv, :], AF.Exp), reads=[tps[b]], writes=[t_PT[pi]])
        return PTs[pi], t_PT[pi]

    for qb in range(NT):
        t0 = qb * 128
        xt = XT3[qb % 2]; t_xt = t_XT3[qb % 2]
        P.dma(DMA(xt, I["x"].ap()[t0:t0 + 128, :]), writes=[t_xt])
        for g in range(2):
            gs = slice(g * 64, (g + 1) * 64)
            qrhs = QT[gs, :, t0:t0 + 128]
            nkt = 1 if 8 * qb + 7 <= 128 else 2
            for nt in range(nkt):
                Mv = min(128, 8 * qb + 7 - 128 * nt)
                s = 128 * nt - 8 * qb + 9 + 239
                PT, tPT = score_exp([(KCT[gs, nt * 128:nt * 128 + Mv], qrhs),
                                     (dband[0:16, s:s + Mv], BTc[0:16, g * 4:(g + 1) * 4, :])], Mv,
                                    [t_KCT, t_QT[qb], t_db, t_BT])
                for r in range(4):
                    st_, sp_ = (nt == 0 and r == 0), (nt == nkt - 1 and r == 3)
                    P.pe(MM(PS[3][:, r * 65:(r + 1) * 65], PT[0:Mv, r * 128:(r + 1) * 128], Vc[0:Mv, nt, g, :],
                            st_, sp_), reads=[tPT, t_Vc], writes=[tps[3]])
                    P.pe(MM(PS[4][:, r * 64:(r + 1) * 64], PT[0:Mv, r * 128:(r + 1) * 128], ovb[0:Mv, nt, :],
                            st_, sp_), reads=[tPT, t_ov], writes=[tps[4]])
            Oc = PS[3][:, 0:260].rearrange("p (r e) -> p r e", r=4)
            P.dve(TS(st[:, 0:4], Oc[:, :, 64], 1e-30, None, ALU.max), reads=[tps[3]], writes=[t_st])
            P.dve(lambda e, o=st[:, 0:4]: e.reciprocal(out=o, in_=o), reads=[t_st], writes=[t_st])
            P.dve(TT(e1, PS[4][:, 0:256].rearrange("p (r s) -> p r s", r=4),
                     st[:, 0:4].unsqueeze(2).to_broadcast([128, 4, 64]), ALU.mult), reads=[tps[4], t_st], writes=[t_e1])
            P.dve(RSUM(score, e1.rearrange("p r s -> p s r")), reads=[t_e1], writes=[t_score])
            P.dve(TT(score, score, arel[:, 63 - 2 * qb:127 - 2 * qb], ALU.add), reads=[t_score, t_arel], writes=[t_score])
            P.dve(TS(score[:, 0:1], score[:, 0:1], 8.0, None, ALU.add), reads=[t_score], writes=[t_score])
            P.dve(lambda e: e.max(out=m8[:, 0:8], in_=score), reads=[t_score], writes=[t_m8])
            P.dve(lambda e: e.match_replace(out=wk, in_to_replace=m8[:, 0:8], in_values=score, imm_value=-3.0e38),
                  reads=[t_m8, t_score], writes=[t_wk])
            P.dve(lambda e: e.max(out=m8[:, 8:16], in_=wk), reads=[t_wk], writes=[t_m8])
            P.dve(TS(negm, score, m8[:, 15:16], NEG, ALU.is_lt, ALU.mult), reads=[t_score, t_m8], writes=[t_negm])
            P.pe(TR(PSB[7][0:64, 0:128], negm, identb), reads=[t_negm, t_identb], writes=[tps[7]])
            P.act(ACT(NMT[0:64, :], PSB[7][0:64, 0:128], AF.Copy), reads=[tps[7]], writes=[t_NMT])
            nmrhs = NMT[0:64, :].unsqueeze(1).to_broadcast([64, 4, 128])
            for kb in range(qb + 1):
                ks = slice(kb * 128, (kb + 1) * 128)
                mms = [(KST[gs, ks], qrhs)]
                if kb == qb:
                    mms.append((identb, BTd[:, g * 4:(g + 1) * 4, :]))
                else:
                    if kb == qb - 1:
                        mms.append((identb, BTo[:, g * 4:(g + 1) * 4, :]))
                    mms.append((Eband[0:64, ks], nmrhs))
                PT, tPT = score_exp(mms, 128, [t_KST[kb], t_QT[qb], t_identb, t_BT, t_Eb, t_NMT])
                for r in range(4):
                    P.pe(MM(PS[5][:, r * 65:(r + 1) * 65], PT[:, r * 128:(r + 1) * 128], Vs[:, kb, g, :],
                            kb == 0 and r == 0, kb == qb and r == 3), reads=[tPT, t_Vs[kb], t_ones], writes=[tps[5]])
            kb0 = max(0, qb - 4)
            for kb in range(kb0, qb + 1):
                ks = slice(kb * 128, (kb + 1) * 128)
                mms = [(KWT[gs, ks], qrhs)]
                if kb == qb:
                    mms.append((identb, BTd[:, g * 4:(g + 1) * 4, :]))
                elif kb == qb - 1:
                    mms.append((identb, BTo[:, g * 4:(g + 1) * 4, :]))
                elif kb == qb - 4:
                    mms.append((identb, edge.unsqueeze(1).to_broadcast([128, 4, 128])))
                PT, tPT = score_exp(mms, 128, [t_KWT[kb], t_QT[qb], t_identb, t_BT, t_edge])
                for r in range(4):
                    P.pe(MM(PS[6][:, r * 65:(r + 1) * 65], PT[:, r * 128:(r + 1) * 128], Vw[:, kb, g, :],
                            kb == kb0 and r == 0, kb == qb and r == 3), reads=[tPT, t_Vw[kb], t_ones], writes=[tps[6]])
            Os = PS[5][:, 0:260].rearrange("p (r e) -> p r e", r=4)
            Ow = PS[6][:, 0:260].rearrange("p (r e) -> p r e", r=4)
            P.dve(lambda e, o=st[:, 4:8], i=Os[:, :, 64]: e.reciprocal(out=o, in_=i), reads=[tps[5]], writes=[t_st])
            P.dve(lambda e, o=st[:, 8:12], i=Ow[:, :, 64]: e.reciprocal(out=o, in_=i), reads=[tps[6]], writes=[t_st])
            gv = gates[:, qb, g * 12:(g + 1) * 12].rearrange("p (r b) -> p b r", b=3)
            P.dve(TT(st[:, 12:24].rearrange("p (b r) -> p b r", b=3), st[:, 0:12].rearrange("p (b r) -> p b r", b=3),
                     gv, ALU.mult), reads=[t_st, t_gates[qb]], writes=[t_st])
            P.dve(TT(e1, Oc[:, :, 0:64], st[:, 12:16].unsqueeze(2).to_broadcast([128, 4, 64]), ALU.mult),
                  reads=[tps[3], t_st], writes=[t_e1])
            P.dve(TT(e2, Os[:, :, 0:64], st[:, 16:20].unsqueeze(2).to_broadcast([128, 4, 64]), ALU.mult),
                  reads=[tps[5], t_st], writes=[t_e2])
            P.dve(TT(e1, e1, e2, ALU.add), reads=[t_e1, t_e2], writes=[t_e1])
            P.dve(TT(e2, Ow[:, :, 0:64], st[:, 20:24].unsqueeze(2).to_broadcast([128, 4, 64]), ALU.mult),
                  reads=[tps[6], t_st], writes=[t_e2])
            P.dve(TT(ONSA[:, g * 256:(g + 1) * 256].rearrange("p (r d) -> p r d", r=4), e1, e2, ALU.add),
                  reads=[t_e1, t_e2], writes=[t_ONSA])
        if dbg:
            P.dma(DMA(dmix[t0:t0 + 128, 0:512], ONSA), reads=[t_ONSA])
        for j in range(4):
            P.pe(TR(PSB[7][:, j * 128:(j + 1) * 128], ONSA[:, j * 128:(j + 1) * 128], identb),
                 reads=[t_ONSA, t_identb], writes=[tps[7]])
        P.act(ACT(MIXT, PSB[7][:, 0:512].rearrange("p (j t) -> p j t", j=4), AF.Copy), reads=[tps[7]], writes=[t_MIXT])
        x1 = X1[qb % 2]; t_x1 = t_X1[qb % 2]
        for nh in range(2):
            bank = 3 + nh
            for c in range(8):
                lhsT = MIXT[:, c, :] if c < 4 else OSGT[:, c - 4, t0:t0 + 128]
                P.pe(MM(PS[bank][:, :], lhsT, Wout[:, c, nh * 512:(nh + 1) * 512], c == 0, c == 7),
                     reads=[t_MIXT, t_OSGT[qb], t_Wout], writes=[tps[bank]])
            P.dve(TT(x1[:, nh * 512:(nh + 1) * 512], PS[bank][:, :], xt[:, nh * 512:(nh + 1) * 512], ALU.add),
                  reads=[tps[bank], t_xt], writes=[t_x1])
        P.dma(DMA(x1s.ap()[t0:t0 + 128, :], x1), reads=[t_x1])

    P.barrier()
    if stop_after == 3:
        P.emit()
        return nc, P

    P.set_phase(4)
    A.off = persist_mark if False else 0
    identb4 = A.alloc([128], BF16)
    identf4 = A.alloc([128], F32); nhalf4 = A.alloc([16], F32)
    Wup = A.alloc([8, 2 * DFF], BF16); t_Wup = Tok()
    Wdn = A.alloc([NFC, D], BF16); t_Wdn = Tok()
    gF = A.alloc([D], F32); t_gF = Tok()
    cwT = A.alloc([176], F32); t_cw = Tok(); t_cb = t_cw
    cw = cwT[:, 0:132].rearrange("p (w c) -> p w c", w=3)
    cbias = cwT[:, 132:176]
    S1 = A.alloc([128], F32); S2 = A.alloc([128], F32); t_S = Tok()
    XF = [A.alloc([D], F32) for _ in range(4)]; t_XF = [Tok() for _ in range(4)]
    hb4 = [A.alloc([D], BF16) for _ in range(2)]; t_hb4 = [Tok(), Tok()]
    h2T = [A.alloc([8, 256], BF16) for _ in range(2)]; t_h2T = [Tok(), Tok()]
    halo = A.alloc([2 * NFC, 2], F32); t_halo = [Tok() for _ in range(2 * NFC)]
    Ub = [A.alloc([258], F32) for _ in range(4)]; t_Ub = [Tok() for _ in range(4)]
    acc = [A.alloc([2, 256], F32) for _ in range(2)]; t_acc = [Tok(), Tok()]
    sg = [A.alloc([256], F32) for _ in range(2)]; t_sg = [Tok(), Tok()]
    actT = [A.alloc([256], BF16) for _ in range(3)]; t_actT = [Tok() for _ in range(3)]
    OT = [A.alloc([D], F32) for _ in range(2)]; t_OT = [Tok(), Tok()]
    sm4 = A.alloc([8], F32); t_sm4 = Tok()

    wupv = I["w_up"].ap().rearrange("(k p) n -> p k n", p=128)
    for k in range(8):
        P.dma(DMA(Wup[:, k, :], wupv[:, k, :]), writes=[t_Wup], q="pool")
    wdnv = I["w_down"].ap().rearrange("(c p) n -> p c n", p=128)
    for c in range(0, NFC, 2):
        P.dma(DMA(Wdn[:, c:c + 2, :], wdnv[:, c:c + 2, :]), writes=[t_Wdn], q="pool")
    P.dma(DMA(gF, bcast_dram(I["ffn_norm"], D)), writes=[t_gF])
    cwv = I["conv_w"].ap().rearrange("w (c p) -> (w c) p", p=128)
    P.dma(DMA(S1, cwv[0:128, :]), writes=[t_S])
    P.dma(DMA(S2[0:4, :], cwv[128:132, :]), writes=[t_S])
    P.dma(DMA(S2[4:48, :], I["conv_b"].ap().rearrange("(c p) -> c p", p=128)), writes=[t_S])
    P.pe(TR(PS[0][:, 0:128], S1, identf4), reads=[t_S, t_identf], writes=[tps[0]])
    P.pe(TR(PS[0][:, 128:176], S2[0:48, :], identf4[0:48, 0:48]), reads=[t_S, t_identf], writes=[tps[0]])
    P.act(ACT(cwT, PS[0][:, 0:176], AF.Copy), reads=[tps[0]], writes=[t_cw])
    P.pool(MEMSET(halo, 0.0), writes=t_halo)

    NST = T // 256
    for stl in range(NST):
        hp = stl % 2
        for i in range(2):
            tt0 = stl * 256 + i * 128
            xf = XF[(stl % 2) * 2 + i]; t_xf = t_XF[(stl % 2) * 2 + i]
            P.dma(DMA(xf, x1s.ap()[tt0:tt0 + 128, :]), writes=[t_xf])
            P.act(ACT(hb4[i], xf, AF.Square, accum_out=sm4[:, 0:1]), reads=[t_xf], writes=[t_hb4[i], t_sm4])
            P.dve(TS(sm4[:, 1:2], sm4[:, 0:1], 1.0 / D, EPS, ALU.mult, ALU.add), reads=[t_sm4], writes=[t_sm4])
            P.pool(TT(sm4[:, 2:3], sm4[:, 1:2], nhalf4[:, 0:1], ALU.pow), reads=[t_sm4, t_nhalf], writes=[t_sm4])
            P.dve(STT(hb4[i], xf, sm4[:, 2:3], gF, ALU.mult, ALU.mult), reads=[t_xf, t_sm4, t_gF], writes=[t_hb4[i]])
            for k in range(8):
                P.pe(TR(PSB[6 + i][:, k * 128:(k + 1) * 128], hb4[i][:, k * 128:(k + 1) * 128], identb4),
                     reads=[t_hb4[i], t_identb], writes=[tps[6 + i]])
            P.act(ACT(h2T[hp][:, :, i * 128:(i + 1) * 128], PSB[6 + i][:, :].rearrange("p (k t) -> p k t", k=8), AF.Copy),
                  reads=[tps[6 + i]], writes=[t_h2T[hp]])
        for c in range(NFC):
            bank = 4 + (c % 2)
            for half in range(2):
                fc = half * NFC + c
                for k in range(8):
                    P.pe(MM(PS[bank][:, half * 256:(half + 1) * 256], Wup[:, k, fc * 128:(fc + 1) * 128], h2T[hp][:, k, :],
                            k == 0, k == 7), reads=[t_Wup, t_h2T[hp]], writes=[tps[bank]])
            ai = c % 2
            for half in range(2):
                fc = half * NFC + c
                src = PS[bank][:, half * 256:(half + 1) * 256]
                ui = (2 * c + half) % 4
                Ubuf = Ub[ui]; t_u = t_Ub[ui]
                P.pool(CP(Ubuf[:, 0:2], halo[:, fc, :]), reads=[t_halo[fc]], writes=[t_u])
                P.act(ACT(Ubuf[:, 2:258], src, AF.Copy), reads=[tps[bank]], writes=[t_u])
                P.pool(CP(halo[:, fc, :], Ubuf[:, 256:258]), reads=[t_u], writes=[t_halo[fc]])
                P.act(ACT(acc[ai][:, half, :], src, AF.Identity, scale=cw[:, 2, fc:fc + 1], bias=cbias[:, fc:fc + 1]),
                      reads=[tps[bank], t_cw, t_cb], writes=[t_acc[ai]])
                eng = P.dve
                eng(STT(acc[ai][:, half, :], Ubuf[:, 1:257], cw[:, 1, fc:fc + 1], acc[ai][:, half, :], ALU.mult, ALU.add),
                    reads=[t_u, t_cw, t_acc[ai]], writes=[t_acc[ai]])
                eng(STT(acc[ai][:, half, :], Ubuf[:, 0:256], cw[:, 0, fc:fc + 1], acc[ai][:, half, :], ALU.mult, ALU.add),
                    reads=[t_u, t_cw, t_acc[ai]], writes=[t_acc[ai]])
            P.act(ACT(sg[ai], acc[ai][:, 1, :], AF.Silu), reads=[t_acc[ai]], writes=[t_sg[ai]])
            a3 = c % 3
            P.dve(TT(actT[a3], sg[ai], acc[ai][:, 0, :], ALU.mult), reads=[t_sg[ai], t_acc[ai]], writes=[t_actT[a3]])
            for i in range(2):
                for nh in range(2):
                    bk = i * 2 + nh
                    P.pe(MM(PS[bk][:, :], actT[a3][:, i * 128:(i + 1) * 128], Wdn[:, c, nh * 512:(nh + 1) * 512],
                            c == 0, c == NFC - 1), reads=[t_actT[a3], t_Wdn], writes=[tps[bk]])
        for i in range(2):
            tt0 = stl * 256 + i * 128
            xf = XF[(stl % 2) * 2 + i]; t_xf = t_XF[(stl % 2) * 2 + i]
            ot = OT[i]
            for nh in range(2):
                bk = i * 2 + nh
                P.dve(TT(ot[:, nh * 512:(nh + 1) * 512], PS[bk][:, :], xf[:, nh * 512:(nh + 1) * 512], ALU.add),
                      reads=[tps[bk], t_xf], writes=[t_OT[i]])
            P.dma(DMA(out.ap()[tt0:tt0 + 128, :], ot), reads=[t_OT[i]])
    P.emit()
    return nc, P


_CACHE = {}


def kernel(**inputs):
    if "nc" not in _CACHE:
        _CACHE["nc"] = build_nc()[0]
        _CACHE["consts"] = host_consts()
    nc = _CACHE["nc"]
    consts = _CACHE["consts"]
    shared = {}
    for k in IN_SHAPES:
        if k == "x":
            continue
        a = np.asarray(inputs[k], dtype=np.float32)
        shared[k] = np.ascontiguousarray(a.reshape(IN_SHAPES[k]))
    shared.update(consts)
    x = np.asarray(inputs["x"], dtype=np.float32)
    in_maps = []
    for b in range(8):
        m = dict(shared)
        m["x"] = np.ascontiguousarray(x[b])
        in_maps.append(m)
    res = run_bass_kernel_spmd(nc, in_maps, core_ids=list(range(8)))
    return np.stack([np.asarray(r["out"], dtype=np.float32) for r in res.results], axis=0)
```

```python
import contextlib
import os
import numpy as np
import concourse.bass as bass
import concourse.mybir as mybir
from concourse.bass_utils import run_bass_kernel_spmd

F32 = mybir.dt.float32
BF16 = mybir.dt.bfloat16
AF = mybir.ActivationFunctionType
ALU = mybir.AluOpType
AX = mybir.AxisListType

T = 4096
D = 1024
NT = T // 128
DFF = 2816
NFC = DFF // 128
EPS = 1e-6
NEG = -30000.0
DF = 384

ENGS = ("pe", "act", "dve", "pool", "sp")
PH_FREE_DEFAULT = "1,3,4"


class Tok:
    __slots__ = ("w", "r", "x")

    def __init__(self, x=False):
        self.w = None
        self.r = []
        self.x = x


class Prog:
    NDMA = 12

    def __init__(self, nc):
        self.nc = nc
        self.ops = []
        self.last = {e: None for e in ENGS}
        self.dmas = []

    def op(self, eng, fn, reads=(), writes=(), dma=False, nss=False, extra=()):
        import os
        if fn is not None and len(self.ops) >= int(os.environ.get("MAXOPS", "100000000")):
            return -1
        oid = len(self.ops)
        deps = set(extra)
        if any(t.x for t in reads):
            writes = list(writes) + [t for t in reads if t.x]
            reads = [t for t in reads if not t.x]
        if eng in ("act", "dve") and fn is not None and any(t.x for t in writes) and os.environ.get("PSUM_GX", "1") == "1":
            if getattr(self, "psum_g", None) is None:
                self.psum_g = Tok()
            writes = list(writes) + [self.psum_g]
        smode = getattr(self, "mode", None) or os.environ.get("SERIAL", "2")
        if smode == "1" or (smode in ("2", "s2pe") and not dma) or (smode == "3"):
            chain = (smode != "3") or dma or getattr(self, "prev_was_dma", False)
            pe_run = (smode == "s2pe" and eng == "pe" and fn is not None and getattr(self, "prev_eng", None) == "pe")
            if pe_run:
                deps.update(getattr(self, "pe_run_deps", ()))
            elif chain and getattr(self, "prev_real", None) is not None:
                deps.add(self.prev_real)
                if smode == "s2pe" and eng == "pe":
                    self.pe_run_deps = [self.prev_real]
            if smode != "3" and not (smode == "s2pe" and eng == "pe"):
                nss = False
            if fn is not None:
                self.prev_real = oid
                self.prev_was_dma = dma
                self.prev_eng = eng
        if smode in ("r1", "r2") and not dma and fn is not None:
            cls = eng if smode == "r1" else ("P" if eng == "pe" else "E")
            if cls != getattr(self, "cur_cls", None):
                self.run_deps = [v for v in self.last.values() if v is not None]
                self.cur_cls = cls
            deps.update(self.run_deps)
        if smode == "dvefence" and not dma and fn is not None:
            if eng in ("dve", "pool"):
                deps.update(v for v in self.last.values() if v is not None)
                self.fence = oid
            elif getattr(self, "fence", None) is not None:
                deps.add(self.fence)
        if smode == "self" and not dma and fn is not None:
            nss = False
            if self.last[eng] is not None:
                deps.add(self.last[eng])
        if smode == "4":
            nss = False
        if smode.startswith("c:") and not dma and eng in smode[2:].split(",") and int(os.environ.get("CH_LO", "0")) <= oid < int(os.environ.get("CH_HI", "100000000")):
            nss = False
            if getattr(self, "prev_c", None) is not None:
                deps.add(self.prev_c)
            if fn is not None:
                self.prev_c = oid
        if smode == "5" and eng in ("act", "dve", "pool") and not dma:
            if getattr(self, "prev_ew", None) is not None:
                deps.add(self.prev_ew)
            if fn is not None:
                self.prev_ew = oid
        for t in reads:
            if t.w is not None:
                deps.add(t.w)
        for t in writes:
            if t.w is not None:
                deps.add(t.w)
            deps.update(t.r)
        deps.discard(oid)
        if fn is not None and not dma:
            if eng == "pool" and self.last["dve"] is not None:
                deps.add(self.last["dve"])
            if eng == "dve" and self.last["pool"] is not None:
                deps.add(self.last["pool"])
        self.ops.append(dict(eng=eng, fn=fn, deps=deps, dma=dma, nss=nss))
        for t in reads:
            t.r.append(oid)
        for t in writes:
            t.w = oid
            t.r = []
        if dma:
            self.dmas.append(oid)
        else:
            if fn is not None:
                self.last[eng] = oid
        return oid

    def pe(self, fn, reads=(), writes=()):
        return self.op("pe", fn, reads, writes, nss=True)

    def act(self, fn, reads=(), writes=()):
        return self.op("act", fn, reads, writes)

    def dve(self, fn, reads=(), writes=()):
        return self.op("dve", fn, reads, writes)

    def pool(self, fn, reads=(), writes=()):
        return self.op("pool", fn, reads, writes)

    def dma(self, fn, reads=(), writes=(), q="sp"):
        return self.op(q, fn, reads, writes, dma=True)

    def set_phase(self, ph):
        free = os.environ.get("PH_FREE", PH_FREE_DEFAULT).split(",")
        self.mode = "0" if str(ph) in free else "2"
        if ph == 3 and "3f" in free:
            self.mode = "dvefence"
        if ph == 3 and "3p" in free:
            self.mode = "s2pe"

    def barrier(self):
        deps = [v for v in self.last.values() if v is not None] + list(self.dmas)
        self.dmas = []
        for e in ENGS:
            self.op(e, None, extra=deps)

    def emit(self):
        nc = self.nc
        ops = self.ops
        n = len(ops)

        def skip(o, od):
            return od["eng"] == o["eng"] and not o["dma"] and o["nss"] and o["fn"] is not None

        needed = [False] * n
        for o in ops:
            for d in o["deps"]:
                od = ops[d]
                if od["dma"] or skip(o, od):
                    continue
                needed[d] = True
        cnt = {e: 0 for e in ENGS}
        signo = [0] * n
        dma_cnt = {"sp": 0, "pool": 0}
        dma_slot = [None] * n
        for i, o in enumerate(ops):
            if o["dma"]:
                di = dma_cnt[o["eng"]]
                dma_slot[i] = ((o["eng"], di % self.NDMA), 16 * (di // self.NDMA + 1))
                dma_cnt[o["eng"]] += 1
            elif needed[i]:
                cnt[o["eng"]] += 1
                signo[i] = cnt[o["eng"]]
        self.stats = dict(cnt=dict(cnt), ndma=dict(dma_cnt), nops=n)
        es = contextlib.ExitStack()
        with es:
            sems = {e: es.enter_context(nc.semaphore("s_" + e)) for e in ENGS}
            dsems = {(q, k): es.enter_context(nc.semaphore("d%s_%d" % (q, k))) for q in ("sp", "pool") for k in range(self.NDMA)}
            block = es.enter_context(nc.Block())
            per_eng = {e: [] for e in ENGS}
            seen = {e: {} for e in ENGS}
            for i, o in enumerate(ops):
                e = o["eng"]
                waits = []

                def need(key, semh, val):
                    if seen[e].get(key, 0) >= val:
                        return
                    seen[e][key] = val
                    waits.append((semh, val))

                for d in sorted(o["deps"]):
                    od = ops[d]
                    if od["dma"]:
                        k, v = dma_slot[d]
                        need(("d", k), dsems[k], v)
                    else:
                        if skip(o, od):
                            continue
                        need(("e", od["eng"]), sems[od["eng"]], signo[d])
                inc = None
                if o["dma"]:
                    k, v = dma_slot[i]
                    if v > 16:
                        need(("d", k), dsems[k], v - 16)
                    inc = (dsems[k], 16)
                elif needed[i]:
                    inc = (sems[e], 1)
                per_eng[e].append((waits, o["fn"], inc))
            tail = []
            for q in ("sp", "pool"):
                for k in range(self.NDMA):
                    if dma_cnt[q] > k:
                        tail.append((dsems[(q, k)], 16 * ((dma_cnt[q] - 1 - k) // self.NDMA + 1)))
            tail += [(sems[e], cnt[e]) for e in ENGS if cnt[e] > 0]

            def runner(ename):
                def body(eng):
                    for waits, fn, inc in per_eng[ename]:
                        for semh, val in waits:
                            eng.wait_ge(semh, val)
                        if fn is None:
                            assert inc is None
                            continue
                        ins = fn(eng)
                        if inc is not None:
                            ins.then_inc(inc[0], inc[1])
                    if ename == "sp":
                        for semh, val in tail:
                            eng.wait_ge(semh, val)
                return body

            block.tensor(runner("pe"))
            block.scalar(runner("act"))
            block.vector(runner("dve"))
            block.gpsimd(runner("pool"))
            block.sync(runner("sp"))


def MM(out, lhsT, rhs, start=True, stop=True):
    return lambda e: e.matmul(out, lhsT=lhsT, rhs=rhs, start=start, stop=stop)


def TR(out, in_, ident):
    return lambda e: e.transpose(out=out, in_=in_, identity=ident)


def ACT(out, in_, func, **kw):
    return lambda e: e.activation(out=out, in_=in_, func=func, **kw)


def TT(out, a, b, op):
    return lambda e: e.tensor_tensor(out=out, in0=a, in1=b, op=op)


def TS(out, a, s1, s2, op0, op1=None):
    if op1 is None:
        return lambda e: e.tensor_scalar(out=out, in0=a, scalar1=s1, scalar2=None, op0=op0)
    return lambda e: e.tensor_scalar(out=out, in0=a, scalar1=s1, scalar2=s2, op0=op0, op1=op1)


def STT(out, a, s, b, op0, op1):
    return lambda e: e.scalar_tensor_tensor(out=out, in0=a, scalar=s, in1=b, op0=op0, op1=op1)


def CP(out, in_):
    return lambda e: e.tensor_copy(out=out, in_=in_)


def DMA(out, in_, **kw):
    return lambda e: e.dma_start(out=out, in_=in_, **kw)


def DMAN(out, in_):
    return lambda e: e.dma_start(out=out, in_=in_, allow_slow_non_contiguous=True)


def MEMSET(ap, v):
    return lambda e: e.memset(ap, v)


def RSUM(out, in_):
    return lambda e: e.reduce_sum(out=out, in_=in_, axis=AX.X)


class Arena:
    def __init__(self, nc, words):
        self.t = nc.alloc_sbuf_tensor("arena", [128, words], F32)
        self.words = words
        self.off = 0

    def alloc(self, free, dt):
        n = int(np.prod(free))
        w = n if dt == F32 else (n + 1) // 2
        w = (w + 7) // 8 * 8
        assert self.off + w <= self.words, ("SBUF arena overflow", self.off, w, self.words)
        v = self.t[:, self.off:self.off + w]
        self.off += w
        if dt != F32:
            v = v.bitcast(dt)
        v = v[:, 0:n]
        if len(free) == 2:
            v = v.rearrange("p (a b) -> p a b", a=free[0])
        elif len(free) == 3:
            v = v.rearrange("p (a b c) -> p a b c", a=free[0], b=free[1])
        return v


def _bucket_np(dist):
    dist = np.asarray(dist, dtype=np.int64)
    d = np.maximum(dist, 1).astype(np.float32)
    lg = (np.log(d / np.float32(16)) / np.float32(np.log(128 / 16)) * np.float32(16)).astype(np.float32)
    log_b = 16 + lg.astype(np.int32)
    log_b = np.clip(log_b, 16, 31)
    return np.where(dist < 16, np.maximum(dist, 0), log_b)


def host_consts():
    c = {}
    c["c_ident"] = np.eye(128, dtype=np.float32)
    c["c_anti"] = np.eye(128, dtype=np.float32)[::-1].copy()
    ohd = np.zeros((33, DF), np.float32)
    for idx in range(DF):
        d = idx - 128
        if d < 0:
            ohd[32, idx] = NEG
        else:
            ohd[_bucket_np(d), idx] += 1.0
            ohd[31, idx] -= 1.0
    c["c_ohd"] = ohd
    eb = np.zeros((64, T), np.float32)
    for s in range(64):
        eb[s, s * 64:(s + 1) * 64] = 1.0
    c["c_eband"] = eb
    db = np.zeros((16, 376), np.float32)
    for k in range(16):
        db[k, k + 239] = 1.0
    c["c_dband"] = db
    ov = np.zeros((256, 64), np.float32)
    for n in range(255):
        for s in range(64):
            if 16 * n < 64 * s + 64 and 16 * n + 32 > 64 * s:
                ov[n, s] = 1.0
    c["c_ov"] = ov
    ar = np.zeros((128, 127), np.float32)
    for q in range(128):
        hi = 1 if q >= 64 else 0
        for j in range(127):
            sp = j - 63
            if sp > hi:
                ar[q, j] = -1e30
            elif sp == hi or sp == hi - 1:
                ar[q, j] = 8.0
    c["c_arel"] = ar
    ed = np.full((128, 128), NEG, np.float32)
    for k in range(128):
        ed[k, :k] = 0.0
    c["c_edge"] = ed
    c["c_tril"] = np.tril(np.ones((128, 128), np.float32))
    return c


CONST_SHAPES = dict(c_ident=[128, 128], c_anti=[128, 128], c_ohd=[33, DF], c_eband=[64, T], c_dband=[16, 376],
                    c_ov=[256, 64], c_arel=[128, 127], c_edge=[128, 128], c_tril=[128, 128])

IN_SHAPES = dict(
    x=[T, D], rel_bias=[32, 8], attn_norm=[D], w_in=[D, 2328], q_norm=[64], k_norm_cmp=[64], k_norm_slc=[64],
    k_norm_win=[64], cmp_pe_k=[32, 64], cmp_w1_k=[2048, 256], cmp_b1_k=[256], cmp_w2_k=[256, 64],
    cmp_pe_v=[32, 64], cmp_w1_v=[2048, 256], cmp_b1_v=[256], cmp_w2_v=[256, 64], sgu_norm=[512],
    sgu_w=[8, 128, 128], sgu_b=[8, 128], w_out=[D, D], ffn_norm=[D], w_up=[D, 2 * DFF], conv_w=[3, 2 * DFF],
    conv_b=[2 * DFF], w_down=[DFF, D])


def build_nc(dbg=False, stop_after=4, p1_tiles=NT):
    nc = bass.Bass("TRN2", target_bir_lowering=False)
    I = {k: nc.dram_tensor(k, s, F32, kind="ExternalInput") for k, s in IN_SHAPES.items()}
    C = {k: nc.dram_tensor(k, s, F32, kind="ExternalInput") for k, s in CONST_SHAPES.items()}
    out = nc.dram_tensor("out", [T, D], F32, kind="ExternalOutput")
    x1s = nc.dram_tensor("x1s", [T, D], F32, kind="ExternalOutput" if dbg else "Internal")
    fscr = nc.dram_tensor("fscr", [8, DF], F32, kind="Internal")
    dmix = nc.dram_tensor("dmix", [T, D], BF16, kind="ExternalOutput") if dbg else None
    dkc = nc.dram_tensor("dkc", [128, 256], BF16, kind="ExternalOutput") if dbg else None
    dvc = nc.dram_tensor("dvc", [128, 260], BF16, kind="ExternalOutput") if dbg else None

    P = Prog(nc)
    A = Arena(nc, 52000)
    PS = [nc.alloc_psum_tensor("ps%d" % i, [128, 512], F32) for i in range(8)]
    PSB = [p.bitcast(BF16) for p in PS]
    tps = [Tok(x=True) for _ in range(8)]

    def bcast_dram(t, n):
        return bass.AP(t, 0, [[0, 128], [1, n]])

    identb = A.alloc([128], BF16); t_identb = Tok()
    identf = A.alloc([128], F32); t_identf = Tok()
    nhalf = A.alloc([16], F32); t_nhalf = Tok()
    QT = A.alloc([4, T], BF16); t_QT = [Tok() for _ in range(NT)]
    KST = A.alloc([T], BF16); t_KST = [Tok() for _ in range(NT)]
    KWT = A.alloc([T], BF16); t_KWT = [Tok() for _ in range(NT)]
    KCc = A.alloc([T], BF16); t_KCc = [Tok() for _ in range(NT)]
    VCc = A.alloc([T], BF16); t_VCc = [Tok() for _ in range(NT)]
    Vs = A.alloc([NT, 2, 65], BF16); t_Vs = [Tok() for _ in range(NT)]
    Vw = A.alloc([NT, 2, 65], BF16); t_Vw = [Tok() for _ in range(NT)]
    gates = A.alloc([NT, 24], F32); t_gates = [Tok() for _ in range(NT)]
    OSGT = A.alloc([4, T], BF16); t_OSGT = [Tok() for _ in range(NT)]
    KCT = A.alloc([256], BF16); t_KCT = Tok()
    Vc = A.alloc([2, 2, 65], BF16); t_Vc = Tok()
    gfc = A.alloc([64], F32); t_gfc = Tok()
    GF4 = A.alloc([4, 64], F32); t_GF4 = Tok()
    persist_mark = A.off

    P.dma(DMA(identf, C["c_ident"].ap()), writes=[t_identf])
    P.dma(DMA(identb, C["c_ident"].ap()), writes=[t_identb], q="pool")
    P.pool(MEMSET(nhalf, -0.5), writes=[t_nhalf])
    t_ones = Tok()
    P.pool(MEMSET(Vs[:, :, :, 64:65], 1.0), writes=[t_ones])
    P.pool(MEMSET(Vw[:, :, :, 64:65], 1.0), writes=[t_ones])
    P.pool(MEMSET(Vc, 0.0), writes=[t_Vc])
    P.pool(MEMSET(Vc[:, :, :, 64:65], 1.0), writes=[t_Vc])
    P.pool(MEMSET(KCT, 0.0), writes=[t_KCT])
    gq = A.alloc([64], F32); gk = A.alloc([3, 64], F32); t_g = Tok()
    P.dma(DMA(gq, bcast_dram(I["q_norm"], 64)), writes=[t_g])
    for i, nm in enumerate(("k_norm_cmp", "k_norm_slc", "k_norm_win")):
        P.dma(DMA(gk[:, i, :], bcast_dram(I[nm], 64)), writes=[t_g])
    P.dve(STT(gfc, gk[:, 0, :], 0.125, gq, ALU.mult, ALU.mult), reads=[t_g], writes=[t_gfc])
    for j, i in enumerate((1, 1, 2, 2)):
        P.dve(STT(GF4[:, j, :], gk[:, i, :], 0.125, gq, ALU.mult, ALU.mult), reads=[t_g], writes=[t_GF4])

    def rsqrt_mean(ss, n_out, nmean, tmp, out, t_in, t_out):
        P.dve(TS(tmp, ss, 1.0 / nmean, EPS, ALU.mult, ALU.add), reads=t_in, writes=t_out)
        P.pool(TT(out, tmp, nhalf[:, 0:n_out], ALU.pow), reads=t_out + [t_nhalf], writes=t_out)

    P.set_phase(1)
    ph_mark = A.off
    Win = A.alloc([8, 2328], BF16); t_Win = Tok()
    gA = A.alloc([D], F32); t_gA = Tok()
    gSG = A.alloc([512], F32); t_gSG = Tok()
    WT = A.alloc([8, 128], BF16); t_WT = Tok()
    sbT = A.alloc([8], F32); t_sbT = Tok()
    XT = [A.alloc([D], F32) for _ in range(2)]; t_XT = [Tok(), Tok()]
    hb = A.alloc([D], BF16); t_hb = Tok()
    hT = A.alloc([8, 128], BF16); t_hT = Tok()
    qf = A.alloc([512], F32); t_qf = Tok()
    sq = A.alloc([512], F32); t_sq = Tok()
    kf = A.alloc([256], F32); t_kf = Tok()
    qb16 = A.alloc([512], BF16); t_qb16 = Tok()
    kb16 = A.alloc([256], BF16); t_kb16 = Tok()
    cb16 = A.alloc([256], BF16); t_cb16 = Tok()
    uf = A.alloc([512], F32); t_uf = Tok()
    vf = A.alloc([512], F32); t_vf = Tok()
    vb = A.alloc([512], BF16); t_vb = Tok()
    zf = sq; t_zf = t_sq
    osg = A.alloc([512], BF16); t_osg = Tok()
    sm = A.alloc([64], F32); t_sm = Tok()
    gt = A.alloc([24], F32); t_gt = Tok()
    sgw = A.alloc([128], F32); t_sgw = Tok()
    tril = A.alloc([128], F32); t_tril = Tok()

    w_in_v = I["w_in"].ap().rearrange("(k p) n -> p k n", p=128)
    col = 0
    segs = []
    for r in range(4):
        for g in range(2):
            segs.append(((g * 4 + r) * 64, 64))
    segs += [(768, 128), (1024, 128), (512, 128), (640, 128)]
    segs += [(896, 128), (1152, 128), (1280, 24)]
    segs += [(1304, 1024)]
    for (c0, n) in segs:
        P.dma(DMA(Win[:, :, col:col + n], w_in_v[:, :, c0:c0 + n]), writes=[t_Win], q="pool")
        col += n
    assert col == 2328
    P.dma(DMA(gA, bcast_dram(I["attn_norm"], D)), writes=[t_gA])
    P.dma(DMA(gSG, bcast_dram(I["sgu_norm"], 512)), writes=[t_gSG])
    P.dma(DMA(tril, C["c_tril"].ap()), writes=[t_tril])
    P.dma(DMAN(sbT, I["sgu_b"].ap().rearrange("g i -> i g")), writes=[t_sbT])
    for g in range(8):
        P.dma(DMA(sgw, I["sgu_w"].ap()[g]), writes=[t_sgw])
        P.dve(TT(sgw, sgw, tril, ALU.mult), reads=[t_sgw, t_tril], writes=[t_sgw])
        P.pe(TR(PS[7][:, 0:128], sgw, identf), reads=[t_sgw, t_identf], writes=[tps[7]])
        P.act(ACT(WT[:, g, :], PS[7][:, 0:128], AF.Copy), reads=[tps[7]], writes=[t_WT])

    CH = [(0, 512), (512, 512), (1024, 280), (1304, 512), (1816, 512)]
    print("ops before tiles", len(P.ops))
    for qb in range(p1_tiles):
        t0 = qb * 128
        print("tile", qb, "starts at op", len(P.ops))
        xt = XT[qb % 2]; t_xt = t_XT[qb % 2]
        P.dma(DMA(xt, I["x"].ap()[t0:t0 + 128, :]), writes=[t_xt])
        P.act(ACT(hb, xt, AF.Square, accum_out=sm[:, 0:1]), reads=[t_xt], writes=[t_hb, t_sm])
        rsqrt_mean(sm[:, 0:1], 1, D, sm[:, 1:2], sm[:, 2:3], [t_sm], [t_sm])
        P.dve(STT(hb, xt, sm[:, 2:3], gA, ALU.mult, ALU.mult), reads=[t_xt, t_sm, t_gA], writes=[t_hb])
        for k in range(8):
            P.pe(TR(PSB[0][:, k * 128:(k + 1) * 128], hb[:, k * 128:(k + 1) * 128], identb),
                 reads=[t_hb, t_identb], writes=[tps[0]])
        P.act(ACT(hT, PSB[0][:, :].rearrange("p (k t) -> p k t", k=8), AF.Copy), reads=[tps[0]], writes=[t_hT])
        for ci, (c0, n) in enumerate(CH):
            for k in range(8):
                P.pe(MM(PS[1 + ci][:, 0:n], hT[:, k, :], Win[:, k, c0:c0 + n], k == 0, k == 7),
                     reads=[t_hT, t_Win], writes=[tps[1 + ci]])
        P.act(ACT(qf, PS[1][:, :], AF.Copy), reads=[tps[1]], writes=[t_qf])
        P.dve(TT(sq, qf, qf, ALU.mult), reads=[t_qf], writes=[t_sq])
        P.dve(RSUM(sm[:, 8:16], sq.rearrange("p (h d) -> p h d", h=8)), reads=[t_sq], writes=[t_sm])
        P.act(ACT(kf, PS[2][:, 0:256], AF.Copy), reads=[tps[2]], writes=[t_kf])
        P.dve(TT(sq[:, 0:256], kf, kf, ALU.mult), reads=[t_kf], writes=[t_sq])
        P.dve(RSUM(sm[:, 16:20], sq[:, 0:256].rearrange("p (h d) -> p h d", h=4)), reads=[t_sq], writes=[t_sm])
        rsqrt_mean(sm[:, 8:20], 12, 64, sm[:, 20:32], sm[:, 32:44], [t_sm], [t_sm])
        P.dve(TT(qb16.rearrange("p (h d) -> p h d", h=8), qf.rearrange("p (h d) -> p h d", h=8),
                 sm[:, 32:40].unsqueeze(2).to_broadcast([128, 8, 64]), ALU.mult), reads=[t_qf, t_sm], writes=[t_qb16])
        P.dve(TT(kf.rearrange("p (h d) -> p h d", h=4), kf.rearrange("p (h d) -> p h d", h=4),
                 sm[:, 40:44].unsqueeze(2).to_broadcast([128, 4, 64]), ALU.mult), reads=[t_kf, t_sm], writes=[t_kf])
        P.dve(TT(kb16, kf, GF4.rearrange("p a b -> p (a b)"), ALU.mult), reads=[t_kf, t_GF4], writes=[t_kb16])
        P.act(ACT(cb16, PS[2][:, 256:512], AF.Copy), reads=[tps[2]], writes=[t_cb16])
        for j in range(4):
            P.pe(TR(PSB[7][:, j * 128:(j + 1) * 128], qb16[:, j * 128:(j + 1) * 128], identb),
                 reads=[t_qb16, t_identb], writes=[tps[7]])
        for j in range(2):
            P.pe(TR(PSB[7][:, 512 + j * 128:640 + j * 128], kb16[:, j * 128:(j + 1) * 128], identb),
                 reads=[t_kb16, t_identb], writes=[tps[7]])
            P.pe(TR(PSB[7][:, 768 + j * 128:896 + j * 128], cb16[:, j * 128:(j + 1) * 128], identb),
                 reads=[t_cb16, t_identb], writes=[tps[7]])
        P.act(ACT(QT[:, :, t0:t0 + 128], PSB[7][:, 0:512].rearrange("p (j t) -> p j t", j=4), AF.Copy),
              reads=[tps[7]], writes=[t_QT[qb]])
        P.dve(CP(KST[:, t0:t0 + 128], PSB[7][:, 512:640]), reads=[tps[7]], writes=[t_KST[qb]])
        P.dve(CP(KWT[:, t0:t0 + 128], PSB[7][:, 640:768]), reads=[tps[7]], writes=[t_KWT[qb]])
        P.act(ACT(KCc[:, t0:t0 + 128], PSB[7][:, 768:896], AF.Copy), reads=[tps[7]], writes=[t_KCc[qb]])
        P.act(ACT(VCc[:, t0:t0 + 128], PSB[7][:, 896:1024], AF.Copy), reads=[tps[7]], writes=[t_VCc[qb]])
        P.act(ACT(Vs[:, qb, :, 0:64], PS[3][:, 0:128].rearrange("p (g d) -> p g d", g=2), AF.Copy),
              reads=[tps[3], t_ones], writes=[t_Vs[qb]])
        P.act(ACT(Vw[:, qb, :, 0:64], PS[3][:, 128:256].rearrange("p (g d) -> p g d", g=2), AF.Copy),
              reads=[tps[3], t_ones], writes=[t_Vw[qb]])
        P.act(ACT(gt, PS[3][:, 256:280], AF.Tanh, scale=0.5), reads=[tps[3]], writes=[t_gt])
        P.dve(TS(gates[:, qb, :], gt, 0.5, 0.5, ALU.mult, ALU.add), reads=[t_gt], writes=[t_gates[qb]])
        P.act(ACT(uf, PS[4][:, :], AF.Gelu_apprx_tanh), reads=[tps[4]], writes=[t_uf])
        P.act(ACT(vf, PS[5][:, :], AF.Gelu_apprx_tanh), reads=[tps[5]], writes=[t_vf])
        P.act(ACT(zf, vf, AF.Square, accum_out=sm[:, 48:49]), reads=[t_vf], writes=[t_zf, t_sm])
        rsqrt_mean(sm[:, 48:49], 1, 512, sm[:, 49:50], sm[:, 50:51], [t_sm], [t_sm])
        P.dve(STT(vb, vf, sm[:, 50:51], gSG, ALU.mult, ALU.mult), reads=[t_vf, t_sm, t_gSG], writes=[t_vb])
        for g in range(8):
            P.pe(MM(PS[6][:, g * 64:(g + 1) * 64], WT[:, g, :], vb[:, g * 64:(g + 1) * 64]),
                 reads=[t_WT, t_vb], writes=[tps[6]])
        P.dve(TT(zf.rearrange("p (g e) -> p g e", g=8), PS[6][:, :].rearrange("p (g e) -> p g e", g=8),
                 sbT.unsqueeze(2).to_broadcast([128, 8, 64]), ALU.add), reads=[tps[6], t_sbT], writes=[t_zf])
        P.dve(TT(osg, zf, uf, ALU.mult), reads=[t_zf, t_uf], writes=[t_osg])
        for j in range(4):
            P.pe(TR(PSB[0][:, j * 128:(j + 1) * 128], osg[:, j * 128:(j + 1) * 128], identb),
                 reads=[t_osg, t_identb], writes=[tps[0]])
        P.act(ACT(OSGT[:, :, t0:t0 + 128], PSB[0][:, 0:512].rearrange("p (j t) -> p j t", j=4), AF.Copy),
              reads=[tps[0]], writes=[t_OSGT[qb]])
        if dbg:
            P.dma(DMA(dmix[t0:t0 + 128, 512:1024], osg), reads=[t_osg])

    P.barrier()
    if stop_after == 1:
        P.emit()
        return nc, P

    P.set_phase(2)
    A.off = ph_mark
    W1 = A.alloc([32, 256], BF16); t_W1 = Tok()
    W2 = A.alloc([2, 64], BF16); t_W2 = Tok()
    peT = A.alloc([32], BF16); t_peT = Tok()
    b1 = A.alloc([2], F32); t_b1 = Tok()
    cst = A.alloc([2], F32); t_cst = Tok()
    hidT = A.alloc([2, 256], BF16); t_hid = Tok()
    kcf = A.alloc([2, 2, 64], F32); t_kcf = Tok()
    kcs = A.alloc([2, 2, 64], F32); t_kcs = Tok()
    kcb = A.alloc([2, 128], BF16); t_kcb = Tok()
    sm2 = A.alloc([16], F32); t_sm2 = Tok()
    P.pool(MEMSET(hidT, 0.0), writes=[t_hid])
    P.pool(MEMSET(kcf, 0.0), writes=[t_kcf])
    for kv in ("k", "v"):
        src = KCc if kv == "k" else VCc
        t_src = t_KCc if kv == "k" else t_VCc
        w1v = I["cmp_w1_" + kv].ap().rearrange("(j d) h -> d j h", d=64)
        for half in range(2):
            P.dma(DMA(W1[half * 64:(half + 1) * 64, :, :], w1v), writes=[t_W1], q="pool")
        P.dma(DMA(W2, I["cmp_w2_" + kv].ap().rearrange("(h p) d -> p h d", p=128)), writes=[t_W2], q="pool")
        P.dma(DMAN(peT[0:64, :], I["cmp_pe_" + kv].ap().rearrange("j d -> d j")), writes=[t_peT], q="pool")
        P.dma(DMAN(b1, I["cmp_b1_" + kv].ap().rearrange("(h p) -> p h", p=128)), writes=[t_b1])
        for half in range(2):
            for j in range(32):
                P.pe(MM(PS[0][:, half:half + 1], W1[0:64, j, half * 128:(half + 1) * 128], peT[0:64, j:j + 1],
                        j == 0, j == 31), reads=[t_W1, t_peT], writes=[tps[0]])
        P.dve(TT(cst, PS[0][:, 0:2], b1, ALU.add), reads=[tps[0], t_b1], writes=[t_cst])
        for g in range(2):
            for half in range(2):
                bank = 1 + half
                for j in range(32):
                    P.pe(MM(PS[bank][:, 0:255], W1[g * 64:(g + 1) * 64, j, half * 128:(half + 1) * 128],
                            src[g * 64:(g + 1) * 64, j:j + 4065:16], j == 0, j == 31),
                         reads=[t_W1] + t_src, writes=[tps[bank]])
                P.act(ACT(hidT[:, half, 0:255], PS[bank][:, 0:255], AF.Gelu_apprx_tanh, bias=cst[:, half:half + 1]),
                      reads=[tps[bank], t_cst], writes=[t_hid])
            for nt in range(2):
                M = 128 if nt == 0 else 127
                for half in range(2):
                    P.pe(MM(PS[3 + nt][0:M, 0:64], hidT[:, half, nt * 128:nt * 128 + M], W2[:, half, :],
                            half == 0, half == 1), reads=[t_hid, t_W2], writes=[tps[3 + nt]])
                if kv == "v":
                    P.act(ACT(Vc[0:M, nt, g, 0:64], PS[3 + nt][0:M, 0:64], AF.Copy), reads=[tps[3 + nt]], writes=[t_Vc])
                else:
                    P.act(ACT(kcf[0:M, nt, g, :], PS[3 + nt][0:M, 0:64], AF.Copy), reads=[tps[3 + nt]], writes=[t_kcf])
        if kv == "k":
            P.dve(TT(kcs, kcf, kcf, ALU.mult), reads=[t_kcf], writes=[t_kcs])
            P.dve(RSUM(sm2[:, 0:4], kcs.rearrange("p a b c -> p (a b) c")), reads=[t_kcs], writes=[t_sm2])
            rsqrt_mean(sm2[:, 0:4], 4, 64, sm2[:, 4:8], sm2[:, 8:12], [t_sm2], [t_sm2])
            P.dve(TT(kcf.rearrange("p a b c -> p (a b) c"), kcf.rearrange("p a b c -> p (a b) c"),
                     sm2[:, 8:12].unsqueeze(2).to_broadcast([128, 4, 64]), ALU.mult), reads=[t_kcf, t_sm2], writes=[t_kcf])
            P.dve(TT(kcb.rearrange("p a (b c) -> p (a b) c", b=2), kcf.rearrange("p a b c -> p (a b) c"),
                     gfc.unsqueeze(1).to_broadcast([128, 4, 64]), ALU.mult), reads=[t_kcf, t_gfc], writes=[t_kcb])
            for nt in range(2):
                M = 128 if nt == 0 else 127
                P.pe(TR(PSB[5][:, nt * 128:nt * 128 + M], kcb[0:M, nt, :], identb[0:M, 0:M]),
                     reads=[t_kcb, t_identb], writes=[tps[5]])
            P.act(ACT(KCT[:, 0:255], PSB[5][:, 0:255], AF.Copy), reads=[tps[5]], writes=[t_KCT])
    if dbg:
        P.dma(DMA(dkc.ap(), KCT), reads=[t_KCT])
        P.dma(DMA(dvc.ap(), Vc.rearrange("p a b c -> p (a b c)")), reads=[t_Vc])
    P.barrier()
    if stop_after == 2:
        P.emit()
        return nc, P

    P.mode = "2"
    A.off = ph_mark
    Wout = A.alloc([8, D], BF16); t_Wout = Tok()
    Eband = A.alloc([T], BF16); t_Eb = Tok()
    dband = A.alloc([376], BF16); t_db = Tok()
    ovb = A.alloc([2, 64], BF16); t_ov = Tok()
    arel = A.alloc([127], F32); t_arel = Tok()
    edge = A.alloc([128], BF16); t_edge = Tok()
    anti = A.alloc([128], F32); t_anti = Tok()
    rbx = A.alloc([8], F32); t_rbx = Tok()
    ohd = A.alloc([DF], F32); t_ohd = Tok()
    Fs = A.alloc([DF], F32); t_Fs = Tok()
    Hk = A.alloc([8, 128], F32); t_Hk = Tok()
    BTd = A.alloc([8, 128], BF16); BTo = A.alloc([8, 128], BF16); BTc = A.alloc([8, 128], BF16); t_BT = Tok()
    PTs = [A.alloc([512], BF16) for _ in range(4)]; t_PT = [Tok() for _ in range(4)]
    XT3 = [A.alloc([D], F32) for _ in range(2)]; t_XT3 = [Tok(), Tok()]
    X1 = [A.alloc([D], F32) for _ in range(2)]; t_X1 = [Tok(), Tok()]
    ONSA = A.alloc([512], BF16); t_ONSA = Tok()
    MIXT = A.alloc([4, 128], BF16); t_MIXT = Tok()
    NMT = A.alloc([128], BF16); t_NMT = Tok()
    negm = A.alloc([64], BF16); t_negm = Tok()
    e1 = A.alloc([4, 64], F32); t_e1 = Tok()
    e2 = A.alloc([4, 64], F32); t_e2 = Tok()
    score = A.alloc([64], F32); t_score = Tok()
    wk = A.alloc([64], F32); t_wk = Tok()
    m8 = A.alloc([16], F32); t_m8 = Tok()
    st = A.alloc([32], F32); t_st = Tok()

    for c0 in range(0, 8, 2):
        P.dma(DMA(Wout[:, c0:c0 + 2, :], I["w_out"].ap().rearrange("(k p) n -> p k n", p=128)[:, c0:c0 + 2, :]),
              writes=[t_Wout], q="pool")
    P.dma(DMA(Eband[0:64, :], C["c_eband"].ap()), writes=[t_Eb], q="pool")
    P.dma(DMA(dband[0:16, :], C["c_dband"].ap()), writes=[t_db], q="pool")
    P.dma(DMA(ovb, C["c_ov"].ap().rearrange("(t p) s -> p t s", p=128)), writes=[t_ov], q="pool")
    P.dma(DMA(arel, C["c_arel"].ap()), writes=[t_arel])
    P.dma(DMA(edge, C["c_edge"].ap()), writes=[t_edge], q="pool")
    P.dma(DMA(anti, C["c_anti"].ap()), writes=[t_anti])
    P.pool(MEMSET(rbx[32:33, :], 1.0), writes=[t_rbx])
    P.dma(DMA(rbx[0:32, :], I["rel_bias"].ap()), writes=[t_rbx])
    P.dma(DMA(ohd[0:33, :], C["c_ohd"].ap()), writes=[t_ohd])
    P.pe(MM(PS[0][0:8, 0:DF], rbx[0:33, 0:8], ohd[0:33, :]), reads=[t_rbx, t_ohd], writes=[tps[0]])
    P.act(ACT(Fs[0:8, :], PS[0][0:8, 0:DF], AF.Copy), reads=[tps[0]], writes=[t_Fs])
    P.dma(DMA(fscr.ap(), Fs[0:8, :]), reads=[t_Fs], writes=[t_Fs])
    for (c0, BT, rows, pstep) in ((1, BTd, 128, 1), (129, BTo, 128, 1), (1, BTc, 16, 16)):
        P.dma(DMA(Hk[0:rows, :, :], bass.AP(fscr, c0, [[pstep, rows], [DF, 8], [1, 128]])), reads=[t_Fs], writes=[t_Hk])
        for hh in range(2):
            lhsT = anti if rows == 128 else anti[0:16, 112:128]
            P.pe(MM(PS[1 + hh][0:rows, :], lhsT, Hk[0:rows, hh * 4:(hh + 1) * 4, :]), reads=[t_anti, t_Hk],
                 writes=[tps[1 + hh]])
            P.act(ACT(BT[0:rows, hh * 4:(hh + 1) * 4, :], PS[1 + hh][0:rows, :].rearrange("p (h q) -> p h q", h=4),
                      AF.Copy), reads=[tps[1 + hh]], writes=[t_BT])

    P.barrier()
    P.set_phase(3)
    sbank = [0, 1, 2]
    sctr = [0]
    pctr = [0]

    def score_exp(mms, Mv, reads):
        b = sbank[sctr[0] % 3]; sctr[0] += 1
        pi = pctr[0] % 4; pctr[0] += 1
        for i, (lhsT, rhs) in enumerate(mms):
            P.op("pe", MM(PS[b][0:Mv, :].rearrange("p (r q) -> p r q", r=4), lhsT, rhs, i == 0, i == len(mms) - 1),
                 reads, [tps[b]], nss=(i == 0))
        P.act(ACT(PTs[pi][0:Mv, :], PS[b][0:Mv, :], AF.Exp), reads=[tps[b]], writes=[t_PT[pi]])
        return PTs[pi], t_PT[pi]

    for qb in range(NT):
        t0 = qb * 128
        xt = XT3[qb % 2]; t_xt = t_XT3[qb % 2]
        P.dma(DMA(xt, I["x"].ap()[t0:t0 + 128, :]), writes=[t_xt])
        for g in range(2):
            gs = slice(g * 64, (g + 1) * 64)
            qrhs = QT[gs, :, t0:t0 + 128]
            nkt = 1 if 8 * qb + 7 <= 128 else 2
            for nt in range(nkt):
                Mv = min(128, 8 * qb + 7 - 128 * nt)
                s = 128 * nt - 8 * qb + 9 + 239
                PT, tPT = score_exp([(KCT[gs, nt * 128:nt * 128 + Mv], qrhs),
                                     (dband[0:16, s:s + Mv], BTc[0:16, g * 4:(g + 1) * 4, :])], Mv,
                                    [t_KCT, t_QT[qb], t_db, t_BT])
                for r in range(4):
                    st_, sp_ = (nt == 0 and r == 0), (nt == nkt - 1 and r == 3)
                    P.pe(MM(PS[3][:, r * 65:(r + 1) * 65], PT[0:Mv, r * 128:(r + 1) * 128], Vc[0:Mv, nt, g, :],
                            st_, sp_), reads=[tPT, t_Vc], writes=[tps[3]])
                    P.pe(MM(PS[4][:, r * 64:(r + 1) * 64], PT[0:Mv, r * 128:(r + 1) * 128], ovb[0:Mv, nt, :],
                            st_, sp_), reads=[tPT, t_ov], writes=[tps[4]])
            Oc = PS[3][:, 0:260].rearrange("p (r e) -> p r e", r=4)
            P.dve(TS(st[:, 0:4], Oc[:, :, 64], 1e-30, None, ALU.max), reads=[tps[3]], writes=[t_st])
            P.dve(lambda e, o=st[:, 0:4]: e.reciprocal(out=o, in_=o), reads=[t_st], writes=[t_st])
            P.dve(TT(e1, PS[4][:, 0:256].rearrange("p (r s) -> p r s", r=4),
                     st[:, 0:4].unsqueeze(2).to_broadcast([128, 4, 64]), ALU.mult), reads=[tps[4], t_st], writes=[t_e1])
            P.dve(RSUM(score, e1.rearrange("p r s -> p s r")), reads=[t_e1], writes=[t_score])
            P.dve(TT(score, score, arel[:, 63 - 2 * qb:127 - 2 * qb], ALU.add), reads=[t_score, t_arel], writes=[t_score])
            P.dve(TS(score[:, 0:1], score[:, 0:1], 8.0, None, ALU.add), reads=[t_score], writes=[t_score])
            P.dve(lambda e: e.max(out=m8[:, 0:8], in_=score), reads=[t_score], writes=[t_m8])
            P.dve(lambda e: e.match_replace(out=wk, in_to_replace=m8[:, 0:8], in_values=score, imm_value=-3.0e38),
                  reads=[t_m8, t_score], writes=[t_wk])
            P.dve(lambda e: e.max(out=m8[:, 8:16], in_=wk), reads=[t_wk], writes=[t_m8])
            P.dve(TS(negm, score, m8[:, 15:16], NEG, ALU.is_lt, ALU.mult), reads=[t_score, t_m8], writes=[t_negm])
            P.pe(TR(PSB[7][0:64, 0:128], negm, identb), reads=[t_negm, t_identb], writes=[tps[7]])
            P.act(ACT(NMT[0:64, :], PSB[7][0:64, 0:128], AF.Copy), reads=[tps[7]], writes=[t_NMT])
            nmrhs = NMT[0:64, :].unsqueeze(1).to_broadcast([64, 4, 128])
            for kb in range(qb + 1):
                ks = slice(kb * 128, (kb + 1) * 128)
                mms = [(KST[gs, ks], qrhs)]
                if kb == qb:
                    mms.append((identb, BTd[:, g * 4:(g + 1) * 4, :]))
                else:
                    if kb == qb - 1:
                        mms.append((identb, BTo[:, g * 4:(g + 1) * 4, :]))
                    mms.append((Eband[0:64, ks], nmrhs))
                PT, tPT = score_exp(mms, 128, [t_KST[kb], t_QT[qb], t_identb, t_BT, t_Eb, t_NMT])
                for r in range(4):
                    P.pe(MM(PS[5][:, r * 65:(r + 1) * 65], PT[:, r * 128:(r + 1) * 128], Vs[:, kb, g, :],
                            kb == 0 and r == 0, kb == qb and r == 3), reads=[tPT, t_Vs[kb], t_ones], writes=[tps[5]])
            kb0 = max(0, qb - 4)
            for kb in range(kb0, qb + 1):
                ks = slice(kb * 128, (kb + 1) * 128)
                mms = [(KWT[gs, ks], qrhs)]
                if kb == qb:
                    mms.append((identb, BTd[:, g * 4:(g + 1) * 4, :]))
                elif kb == qb - 1:
                    mms.append((identb, BTo[:, g * 4:(g + 1) * 4, :]))
                elif kb == qb - 4:
                    mms.append((identb, edge.unsqueeze(1).to_broadcast([128, 4, 128])))
                PT, tPT = score_exp(mms, 128, [t_KWT[kb], t_QT[qb], t_identb, t_BT, t_edge])
                for r in range(4):
                    P.pe(MM(PS[6][:, r * 65:(r + 1) * 65], PT[:, r * 128:(r + 1) * 128], Vw[:, kb, g, :],
                            kb == kb0 and r == 0, kb == qb and r == 3), reads=[tPT, t_Vw[kb], t_ones], writes=[tps[6]])
            Os = PS[5][:, 0:260].rearrange("p (r e) -> p r e", r=4)
            Ow = PS[6][:, 0:260].rearrange("p (r e) -> p r e", r=4)
            P.dve(lambda e, o=st[:, 4:8], i=Os[:, :, 64]: e.reciprocal(out=o, in_=i), reads=[tps[5]], writes=[t_st])
            P.dve(lambda e, o=st[:, 8:12], i=Ow[:, :, 64]: e.reciprocal(out=o, in_=i), reads=[tps[6]], writes=[t_st])
            gv = gates[:, qb, g * 12:(g + 1) * 12].rearrange("p (r b) -> p b r", b=3)
            P.dve(TT(st[:, 12:24].rearrange("p (b r) -> p b r", b=3), st[:, 0:12].rearrange("p (b r) -> p b r", b=3),
                     gv, ALU.mult), reads=[t_st, t_gates[qb]], writes=[t_st])
            P.dve(TT(e1, Oc[:, :, 0:64], st[:, 12:16].unsqueeze(2).to_broadcast([128, 4, 64]), ALU.mult),
                  reads=[tps[3], t_st], writes=[t_e1])
            P.dve(TT(e2, Os[:, :, 0:64], st[:, 16:20].unsqueeze(2).to_broadcast([128, 4, 64]), ALU.mult),
                  reads=[tps[5], t_st], writes=[t_e2])
            P.dve(TT(e1, e1, e2, ALU.add), reads=[t_e1, t_e2], writes=[t_e1])
            P.dve(TT(e2, Ow[:, :, 0:64], st[:, 20:24].unsqueeze(2).to_broadcast([128, 4, 64]), ALU.mult),
                  reads=[tps[6], t_st], writes=[t_e2])
            P.dve(TT(ONSA[:, g * 256:(g + 1) * 256].rearrange("p (r d) -> p r d", r=4), e1, e2, ALU.add),
                  reads=[t_e1, t_e2], writes=[t_ONSA])
        if dbg:
            P.dma(DMA(dmix[t0:t0 + 128, 0:512], ONSA), reads=[t_ONSA])
        for j in range(4):
            P.pe(TR(PSB[7][:, j * 128:(j + 1) * 128], ONSA[:, j * 128:(j + 1) * 128], identb),
                 reads=[t_ONSA, t_identb], writes=[tps[7]])
        P.act(ACT(MIXT, PSB[7][:, 0:512].rearrange("p (j t) -> p j t", j=4), AF.Copy), reads=[tps[7]], writes=[t_MIXT])
        x1 = X1[qb % 2]; t_x1 = t_X1[qb % 2]
        for nh in range(2):
            bank = 3 + nh
            for c in range(8):
                lhsT = MIXT[:, c, :] if c < 4 else OSGT[:, c - 4, t0:t0 + 128]
                P.pe(MM(PS[bank][:, :], lhsT, Wout[:, c, nh * 512:(nh + 1) * 512], c == 0, c == 7),
                     reads=[t_MIXT, t_OSGT[qb], t_Wout], writes=[tps[bank]])
            P.dve(TT(x1[:, nh * 512:(nh + 1) * 512], PS[bank][:, :], xt[:, nh * 512:(nh + 1) * 512], ALU.add),
                  reads=[tps[bank], t_xt], writes=[t_x1])
        P.dma(DMA(x1s.ap()[t0:t0 + 128, :], x1), reads=[t_x1])

    P.barrier()
    if stop_after == 3:
        P.emit()
        return nc, P

    P.set_phase(4)
    A.off = persist_mark if False else 0
    identb4 = A.alloc([128], BF16)
    identf4 = A.alloc([128], F32); nhalf4 = A.alloc([16], F32)
    Wup = A.alloc([8, 2 * DFF], BF16); t_Wup = Tok()
    Wdn = A.alloc([NFC, D], BF16); t_Wdn = Tok()
    gF = A.alloc([D], F32); t_gF = Tok()
    cwT = A.alloc([176], F32); t_cw = Tok(); t_cb = t_cw
    cw = cwT[:, 0:132].rearrange("p (w c) -> p w c", w=3)
    cbias = cwT[:, 132:176]
    S1 = A.alloc([128], F32); S2 = A.alloc([128], F32); t_S = Tok()
    XF = [A.alloc([D], F32) for _ in range(4)]; t_XF = [Tok() for _ in range(4)]
    hb4 = [A.alloc([D], BF16) for _ in range(2)]; t_hb4 = [Tok(), Tok()]
    h2T = [A.alloc([8, 256], BF16) for _ in range(2)]; t_h2T = [Tok(), Tok()]
    halo = A.alloc([2 * NFC, 2], F32); t_halo = [Tok() for _ in range(2 * NFC)]
    Ub = [A.alloc([258], F32) for _ in range(4)]; t_Ub = [Tok() for _ in range(4)]
    acc = [A.alloc([2, 256], F32) for _ in range(2)]; t_acc = [Tok(), Tok()]
    sg = [A.alloc([256], F32) for _ in range(2)]; t_sg = [Tok(), Tok()]
    actT = [A.alloc([256], BF16) for _ in range(3)]; t_actT = [Tok() for _ in range(3)]
    OT = [A.alloc([D], F32) for _ in range(2)]; t_OT = [Tok(), Tok()]
    sm4 = A.alloc([8], F32); t_sm4 = Tok()

    wupv = I["w_up"].ap().rearrange("(k p) n -> p k n", p=128)
    for k in range(8):
        P.dma(DMA(Wup[:, k, :], wupv[:, k, :]), writes=[t_Wup], q="pool")
    wdnv = I["w_down"].ap().rearrange("(c p) n -> p c n", p=128)
    for c in range(0, NFC, 2):
        P.dma(DMA(Wdn[:, c:c + 2, :], wdnv[:, c:c + 2, :]), writes=[t_Wdn], q="pool")
    P.dma(DMA(gF, bcast_dram(I["ffn_norm"], D)), writes=[t_gF])
    cwv = I["conv_w"].ap().rearrange("w (c p) -> (w c) p", p=128)
    P.dma(DMA(S1, cwv[0:128, :]), writes=[t_S])
    P.dma(DMA(S2[0:4, :], cwv[128:132, :]), writes=[t_S])
    P.dma(DMA(S2[4:48, :], I["conv_b"].ap().rearrange("(c p) -> c p", p=128)), writes=[t_S])
    P.pe(TR(PS[0][:, 0:128], S1, identf4), reads=[t_S, t_identf], writes=[tps[0]])
    P.pe(TR(PS[0][:, 128:176], S2[0:48, :], identf4[0:48, 0:48]), reads=[t_S, t_identf], writes=[tps[0]])
    P.act(ACT(cwT, PS[0][:, 0:176], AF.Copy), reads=[tps[0]], writes=[t_cw])
    P.pool(MEMSET(halo, 0.0), writes=t_halo)

    NST = T // 256
    for stl in range(NST):
        hp = stl % 2
        for i in range(2):
            tt0 = stl * 256 + i * 128
            xf = XF[(stl % 2) * 2 + i]; t_xf = t_XF[(stl % 2) * 2 + i]
            P.dma(DMA(xf, x1s.ap()[tt0:tt0 + 128, :]), writes=[t_xf])
            P.act(ACT(hb4[i], xf, AF.Square, accum_out=sm4[:, 0:1]), reads=[t_xf], writes=[t_hb4[i], t_sm4])
            P.dve(TS(sm4[:, 1:2], sm4[:, 0:1], 1.0 / D, EPS, ALU.mult, ALU.add), reads=[t_sm4], writes=[t_sm4])
            P.pool(TT(sm4[:, 2:3], sm4[:, 1:2], nhalf4[:, 0:1], ALU.pow), reads=[t_sm4, t_nhalf], writes=[t_sm4])
            P.dve(STT(hb4[i], xf, sm4[:, 2:3], gF, ALU.mult, ALU.mult), reads=[t_xf, t_sm4, t_gF], writes=[t_hb4[i]])
            for k in range(8):
                P.pe(TR(PSB[6 + i][:, k * 128:(k + 1) * 128], hb4[i][:, k * 128:(k + 1) * 128], identb4),
                     reads=[t_hb4[i], t_identb], writes=[tps[6 + i]])
            P.act(ACT(h2T[hp][:, :, i * 128:(i + 1) * 128], PSB[6 + i][:, :].rearrange("p (k t) -> p k t", k=8), AF.Copy),
                  reads=[tps[6 + i]], writes=[t_h2T[hp]])
        for c in range(NFC):
            bank = 4 + (c % 2)
            for half in range(2):
                fc = half * NFC + c
                for k in range(8):
                    P.pe(MM(PS[bank][:, half * 256:(half + 1) * 256], Wup[:, k, fc * 128:(fc + 1) * 128], h2T[hp][:, k, :],
                            k == 0, k == 7), reads=[t_Wup, t_h2T[hp]], writes=[tps[bank]])
            ai = c % 2
            for half in range(2):
                fc = half * NFC + c
                src = PS[bank][:, half * 256:(half + 1) * 256]
                ui = (2 * c + half) % 4
                Ubuf = Ub[ui]; t_u = t_Ub[ui]
                P.pool(CP(Ubuf[:, 0:2], halo[:, fc, :]), reads=[t_halo[fc]], writes=[t_u])
                P.act(ACT(Ubuf[:, 2:258], src, AF.Copy), reads=[tps[bank]], writes=[t_u])
                P.pool(CP(halo[:, fc, :], Ubuf[:, 256:258]), reads=[t_u], writes=[t_halo[fc]])
                P.act(ACT(acc[ai][:, half, :], src, AF.Identity, scale=cw[:, 2, fc:fc + 1], bias=cbias[:, fc:fc + 1]),
                      reads=[tps[bank], t_cw, t_cb], writes=[t_acc[ai]])
                eng = P.dve
                eng(STT(acc[ai][:, half, :], Ubuf[:, 1:257], cw[:, 1, fc:fc + 1], acc[ai][:, half, :], ALU.mult, ALU.add),
                    reads=[t_u, t_cw, t_acc[ai]], writes=[t_acc[ai]])
                eng(STT(acc[ai][:, half, :], Ubuf[:, 0:256], cw[:, 0, fc:fc + 1], acc[ai][:, half, :], ALU.mult, ALU.add),
                    reads=[t_u, t_cw, t_acc[ai]], writes=[t_acc[ai]])
            P.act(ACT(sg[ai], acc[ai][:, 1, :], AF.Silu), reads=[t_acc[ai]], writes=[t_sg[ai]])
            a3 = c % 3
            P.dve(TT(actT[a3], sg[ai], acc[ai][:, 0, :], ALU.mult), reads=[t_sg[ai], t_acc[ai]], writes=[t_actT[a3]])
            for i in range(2):
                for nh in range(2):
                    bk = i * 2 + nh
                    P.pe(MM(PS[bk][:, :], actT[a3][:, i * 128:(i + 1) * 128], Wdn[:, c, nh * 512:(nh + 1) * 512],
                            c == 0, c == NFC - 1), reads=[t_actT[a3], t_Wdn], writes=[tps[bk]])
        for i in range(2):
            tt0 = stl * 256 + i * 128
            xf = XF[(stl % 2) * 2 + i]; t_xf = t_XF[(stl % 2) * 2 + i]
            ot = OT[i]
            for nh in range(2):
                bk = i * 2 + nh
                P.dve(TT(ot[:, nh * 512:(nh + 1) * 512], PS[bk][:, :], xf[:, nh * 512:(nh + 1) * 512], ALU.add),
                      reads=[tps[bk], t_xf], writes=[t_OT[i]])
            P.dma(DMA(out.ap()[tt0:tt0 + 128, :], ot), reads=[t_OT[i]])
    P.emit()
    return nc, P


_CACHE = {}


def kernel(**inputs):
    if "nc" not in _CACHE:
        _CACHE["nc"] = build_nc()[0]
        _CACHE["consts"] = host_consts()
    nc = _CACHE["nc"]
    consts = _CACHE["consts"]
    shared = {}
    for k in IN_SHAPES:
        if k == "x":
            continue
        a = np.asarray(inputs[k], dtype=np.float32)
        shared[k] = np.ascontiguousarray(a.reshape(IN_SHAPES[k]))
    shared.update(consts)
    x = np.asarray(inputs["x"], dtype=np.float32)
    in_maps = []
    for b in range(8):
        m = dict(shared)
        m["x"] = np.ascontiguousarray(x[b])
        in_maps.append(m)
    res = run_bass_kernel_spmd(nc, in_maps, core_ids=list(range(8)))
    return np.stack([np.asarray(r["out"], dtype=np.float32) for r in res.results], axis=0)
```
